# Optimizing a Trainium2 kernel written in Bass

```python
import math
import jax
import jax.numpy as jnp
from jax import lax
import numpy as np

D_MODEL = 1024
BATCH = 8
SEQ = 4096
DEPTH = 2
DEC_BATCH = 8
DEC_SEQ = 2048
PAST_LEN = 128

GRID_W = 64
BLOCK = 128
HEAD_DIM = 64
MIX_WIDTH = D_MODEL
EPS = 1e-6
ROPE_THETA = 10000.0

A_HEADS = 4
A_KV = 2
WINDOW = 128
B_HEADS = 4
B_Q_RANK = 256
B_KV_RANK = 128
B_NOPE = 64
B_ROPE = 32
B_V = 64
C_HEADS = 4
C_KV = 2
D_HEADS = 4
D_QK = 32
D_V = 64

N_ALIBI = A_HEADS + D_HEADS

D_FF = 2816
CONV_W = 3

IN_SIZES = (
    A_HEADS * HEAD_DIM, A_KV * HEAD_DIM, A_KV * HEAD_DIM,
    B_Q_RANK, B_KV_RANK, B_ROPE,
    C_HEADS * HEAD_DIM, C_KV * HEAD_DIM, C_KV * HEAD_DIM,
    D_HEADS * 2 * D_QK, D_HEADS * 2 * D_QK, D_HEADS * D_V,
)
IN_WIDTH = sum(IN_SIZES)

kernel_name = "hybrid_parallel_heads_encoder"


def rms_norm(x, g):
    xf = x.astype(jnp.float32)
    y = xf * lax.rsqrt(jnp.mean(xf * xf, axis=-1, keepdims=True) + EPS)
    return (y * g.astype(jnp.float32)).astype(x.dtype)


def rope(x, pos):
    dim = x.shape[-1]
    half = dim // 2
    inv = ROPE_THETA ** (-jnp.arange(half, dtype=jnp.float32) * 2.0 / dim)
    ang = pos[:, None] * inv[None, :]
    cos = jnp.cos(ang)[:, None, :]
    sin = jnp.sin(ang)[:, None, :]
    xf = x.astype(jnp.float32)
    x1, x2 = xf[..., :half], xf[..., half:]
    return jnp.concatenate([x1 * cos - x2 * sin, x2 * cos + x1 * sin], axis=-1).astype(x.dtype)


def sweep_query_blocks(fn, qs):
    bsz, s_len = qs[0].shape[:2]
    nb = s_len // BLOCK
    blocks = tuple(jnp.swapaxes(a.reshape(bsz, nb, BLOCK, *a.shape[2:]), 0, 1) for a in qs)
    out = lax.map(fn, (jnp.arange(nb), *blocks))
    out = jnp.swapaxes(out, 0, 1)
    return out.reshape(bsz, s_len, *out.shape[3:])


def windowed_sink_attention(q, k, v, sink, slopes):
    bsz, s_len = q.shape[:2]
    nb = s_len // BLOCK
    grp = A_HEADS // A_KV
    qb = q.reshape(bsz, nb, BLOCK, A_KV, grp, HEAD_DIM)

    def band(t):
        tp = jnp.pad(t, ((0, 0), (BLOCK, BLOCK), (0, 0), (0, 0))).reshape(bsz, nb + 2, BLOCK, A_KV, HEAD_DIM)
        return jnp.concatenate([tp[:, :-2], tp[:, 1:-1], tp[:, 2:]], axis=2)

    kb, vb = band(k), band(v)
    s = jnp.einsum("bnqkgd,bnjkd->bnkgqj", qb, kb, preferred_element_type=jnp.float32) * (HEAD_DIM ** -0.5)
    blk = jnp.arange(nb)
    qpos = blk[:, None] * BLOCK + jnp.arange(BLOCK)[None, :]
    kpos = (blk[:, None] - 1) * BLOCK + jnp.arange(3 * BLOCK)[None, :]
    dist = jnp.abs(qpos[:, :, None] - kpos[:, None, :])
    valid = (dist <= WINDOW) & (kpos >= 0)[:, None, :] & (kpos < s_len)[:, None, :]
    m_h = slopes.reshape(A_KV, grp)[None, None, :, :, None, None]
    s = s - m_h * dist.astype(jnp.float32)[None, :, None, None]
    s = jnp.where(valid[None, :, None, None], s, -jnp.inf)
    sink_l = sink.astype(jnp.float32).reshape(A_KV, grp)[None, None, :, :, None, None]
    m = jnp.maximum(jnp.max(s, axis=-1, keepdims=True), sink_l)
    p = jnp.exp(s - m)
    p = p / (jnp.sum(p, axis=-1, keepdims=True) + jnp.exp(sink_l - m))
    o = jnp.einsum("bnkgqj,bnjkd->bnqkgd", p.astype(v.dtype), vb)
    return o.reshape(bsz, s_len, A_HEADS * HEAD_DIM)


def latent_attention(c_q, c_kv, k_rope_raw, q_norm, w_q_up, kv_norm, w_kv_up, pos):
    bsz, s_len = c_q.shape[:2]
    q = (rms_norm(c_q, q_norm) @ w_q_up).reshape(bsz, s_len, B_HEADS, B_NOPE + B_ROPE)
    q_nope = q[..., :B_NOPE]
    q_rope = rope(q[..., B_NOPE:], pos)
    kv = (rms_norm(c_kv, kv_norm) @ w_kv_up).reshape(bsz, s_len, B_HEADS, B_NOPE + B_V)
    k_nope, vh = kv[..., :B_NOPE], kv[..., B_NOPE:]
    k_rope = rope(k_rope_raw[:, :, None, :], pos)[:, :, 0]
    scale = (B_NOPE + B_ROPE) ** -0.5

    def block(args):
        _, qn, qr = args
        s = (jnp.einsum("bqhd,bshd->bhqs", qn, k_nope, preferred_element_type=jnp.float32)
             + jnp.einsum("bqhd,bsd->bhqs", qr, k_rope, preferred_element_type=jnp.float32)) * scale
        p = jax.nn.softmax(s, axis=-1).astype(vh.dtype)
        return jnp.einsum("bhqs,bshd->bqhd", p, vh)

    o = sweep_query_blocks(block, (q_nope, q_rope))
    return o.reshape(bsz, s_len, B_HEADS * B_V)


def axial_rope_gqa(q, k, v, q_norm, k_norm, row_pos, col_pos):
    bsz, s_len = q.shape[:2]
    grp = C_HEADS // C_KV
    half = HEAD_DIM // 2

    def prep(t, n_h, g):
        t = rms_norm(t.reshape(bsz, s_len, n_h, HEAD_DIM), g)
        return jnp.concatenate([rope(t[..., :half], row_pos), rope(t[..., half:], col_pos)], axis=-1)

    qh = prep(q, C_HEADS, q_norm).reshape(bsz, s_len, C_KV, grp, HEAD_DIM)
    kh = prep(k, C_KV, k_norm)
    vh = v.reshape(bsz, s_len, C_KV, HEAD_DIM)
    scale = HEAD_DIM ** -0.5

    def block(args):
        _, qb = args
        s = jnp.einsum("bqkgd,bskd->bkgqs", qb, kh, preferred_element_type=jnp.float32) * scale
        p = jax.nn.softmax(s, axis=-1).astype(vh.dtype)
        return jnp.einsum("bkgqs,bskd->bqkgd", p, vh)

    o = sweep_query_blocks(block, (qh,))
    return o.reshape(bsz, s_len, C_HEADS * HEAD_DIM)


def differential_attention(q, k, v, lq1, lk1, lq2, lk2, sub_norm, slopes, lam_init):
    bsz, s_len = q.shape[:2]
    f32 = jnp.float32
    qh = q.reshape(bsz, s_len, D_HEADS, 2, D_QK)
    kh = k.reshape(bsz, s_len, D_HEADS, 2, D_QK)
    vh = v.reshape(bsz, s_len, D_HEADS, D_V)
    lam = (jnp.exp(jnp.sum(lq1.astype(f32) * lk1.astype(f32)))
           - jnp.exp(jnp.sum(lq2.astype(f32) * lk2.astype(f32))) + lam_init)
    kpos = jnp.arange(s_len)
    scale = D_QK ** -0.5

    def block(args):
        i, qb = args
        s = jnp.einsum("bqhcd,bshcd->bhcqs", qb, kh, preferred_element_type=f32) * scale
        qpos = i * BLOCK + jnp.arange(BLOCK)
        dist = jnp.abs(qpos[:, None] - kpos[None, :]).astype(f32)
        s = s - slopes[None, :, None, None, None] * dist
        p = jax.nn.softmax(s, axis=-1)
        a = p[:, :, 0] - lam * p[:, :, 1]
        return jnp.einsum("bhqs,bshd->bqhd", a.astype(vh.dtype), vh)

    o = sweep_query_blocks(block, (qh,))
    o = rms_norm(o, sub_norm) * (1.0 - lam_init)
    return o.reshape(bsz, s_len, D_HEADS * D_V)


def conv_gated_mlp(x, w_up, b_up, conv_w, conv_b, w_down):
    h = x @ w_up + b_up
    hp = jnp.pad(h, ((0, 0), (CONV_W // 2, CONV_W // 2), (0, 0)))
    h = conv_w[0] * hp[:, :-2] + conv_w[1] * hp[:, 1:-1] + conv_w[2] * hp[:, 2:] + conv_b
    a, b = jnp.split(h, 2, axis=-1)
    return (jax.nn.silu(a) * b) @ w_down


def trunk(x, g_attn, w_in, a_sink, b_q_norm, b_w_q_up, b_kv_norm, b_w_kv_up, c_q_norm, c_k_norm,
          d_lambda_q1, d_lambda_k1, d_lambda_q2, d_lambda_k2, d_sub_norm, w_out,
          g_ffn, w_up, b_up, conv_w, conv_b, w_down, g_final):
    bsz, s_len, _ = x.shape
    f32 = jnp.float32
    pos = jnp.arange(s_len, dtype=f32)
    rows = s_len // GRID_W
    row_pos = jnp.broadcast_to(jnp.arange(rows, dtype=f32)[:, None], (rows, GRID_W)).reshape(s_len)
    col_pos = jnp.broadcast_to(jnp.arange(GRID_W, dtype=f32)[None, :], (rows, GRID_W)).reshape(s_len)
    slopes = 2.0 ** (-8.0 * (jnp.arange(N_ALIBI, dtype=f32) + 1.0) / N_ALIBI)
    split_at = np.cumsum(IN_SIZES)[:-1].tolist()
    for l in range(DEPTH):
        h = rms_norm(x, g_attn[l]) @ w_in[l]
        aq, ak, av, bcq, bckv, bkr, cq, ck, cv, dq, dk, dv = jnp.split(h, split_at, axis=-1)
        o_a = windowed_sink_attention(
            aq.reshape(bsz, s_len, A_HEADS, HEAD_DIM),
            ak.reshape(bsz, s_len, A_KV, HEAD_DIM),
            av.reshape(bsz, s_len, A_KV, HEAD_DIM),
            a_sink[l], slopes[:A_HEADS])
        o_b = latent_attention(bcq, bckv, bkr, b_q_norm[l], b_w_q_up[l], b_kv_norm[l], b_w_kv_up[l], pos)
        o_c = axial_rope_gqa(cq, ck, cv, c_q_norm[l], c_k_norm[l], row_pos, col_pos)
        lam_init = 0.8 - 0.6 * math.exp(-0.3 * l)
        o_d = differential_attention(dq, dk, dv, d_lambda_q1[l], d_lambda_k1[l], d_lambda_q2[l], d_lambda_k2[l],
                                     d_sub_norm[l], slopes[A_HEADS:], lam_init)
        x = x + jnp.concatenate([o_a, o_b, o_c, o_d], axis=-1) @ w_out[l]
        x = x + conv_gated_mlp(rms_norm(x, g_ffn[l]), w_up[l], b_up[l], conv_w[l], conv_b[l], w_down[l])
    return rms_norm(x, g_final)


def setup_inputs(seed: int = 0) -> dict:
    key = jax.random.key(seed)
    ks = jax.random.split(key, 24)
    f32 = jnp.float32
    L = DEPTH

    def nrm(k, shape, scale):
        return jax.random.normal(k, shape, f32) * scale

    def gain(k, shape):
        return 1.0 + 0.05 * jax.random.normal(k, shape, f32)

    return {
        "x_prompt": nrm(ks[0], (BATCH, SEQ, D_MODEL), 1.0),
        "x_sample": nrm(ks[1], (DEC_BATCH, DEC_SEQ, D_MODEL), 1.0),
        "g_attn": gain(ks[2], (L, D_MODEL)),
        "w_in": nrm(ks[3], (L, D_MODEL, IN_WIDTH), D_MODEL ** -0.5),
        "a_sink": nrm(ks[4], (L, A_HEADS), 0.5),
        "b_q_norm": gain(ks[5], (L, B_Q_RANK)),
        "b_w_q_up": nrm(ks[6], (L, B_Q_RANK, B_HEADS * (B_NOPE + B_ROPE)), B_Q_RANK ** -0.5),
        "b_kv_norm": gain(ks[7], (L, B_KV_RANK)),
        "b_w_kv_up": nrm(ks[8], (L, B_KV_RANK, B_HEADS * (B_NOPE + B_V)), B_KV_RANK ** -0.5),
        "c_q_norm": gain(ks[9], (L, HEAD_DIM)),
        "c_k_norm": gain(ks[10], (L, HEAD_DIM)),
        "d_lambda_q1": nrm(ks[11], (L, D_QK), 0.1),
        "d_lambda_k1": nrm(ks[12], (L, D_QK), 0.1),
        "d_lambda_q2": nrm(ks[13], (L, D_QK), 0.1),
        "d_lambda_k2": nrm(ks[14], (L, D_QK), 0.1),
        "d_sub_norm": gain(ks[15], (L, D_V)),
        "w_out": nrm(ks[16], (L, MIX_WIDTH, D_MODEL), MIX_WIDTH ** -0.5),
        "g_ffn": gain(ks[17], (L, D_MODEL)),
        "w_up": nrm(ks[18], (L, D_MODEL, 2 * D_FF), D_MODEL ** -0.5),
        "b_up": nrm(ks[19], (L, 2 * D_FF), 0.02),
        "conv_w": nrm(ks[20], (L, CONV_W, 2 * D_FF), CONV_W ** -0.5),
        "conv_b": nrm(ks[21], (L, 2 * D_FF), 0.02),
        "w_down": nrm(ks[22], (L, D_FF, D_MODEL), D_FF ** -0.5),
        "g_final": gain(ks[23], (D_MODEL,)),
    }


def reference(x_prompt, x_sample, g_attn, w_in, a_sink, b_q_norm, b_w_q_up, b_kv_norm, b_w_kv_up,
              c_q_norm, c_k_norm, d_lambda_q1, d_lambda_k1, d_lambda_q2, d_lambda_k2, d_sub_norm, w_out,
              g_ffn, w_up, b_up, conv_w, conv_b, w_down, g_final):
    y_prompt = trunk(x_prompt, g_attn, w_in, a_sink, b_q_norm, b_w_q_up, b_kv_norm, b_w_kv_up,
                     c_q_norm, c_k_norm, d_lambda_q1, d_lambda_k1, d_lambda_q2, d_lambda_k2, d_sub_norm, w_out,
                     g_ffn, w_up, b_up, conv_w, conv_b, w_down, g_final)
    y_sample = trunk(x_sample, g_attn, w_in, a_sink, b_q_norm, b_w_q_up, b_kv_norm, b_w_kv_up,
                     c_q_norm, c_k_norm, d_lambda_q1, d_lambda_k1, d_lambda_q2, d_lambda_k2, d_sub_norm, w_out,
                     g_ffn, w_up, b_up, conv_w, conv_b, w_down, g_final)
    return (y_prompt, y_sample)
```

```python
import contextlib
import math
import numpy as np
import ml_dtypes
import concourse.bass as bass
import concourse.mybir as mybir
from concourse.bass_utils import run_bass_kernel_spmd

F32 = mybir.dt.float32
BF16 = mybir.dt.bfloat16
ALU = mybir.AluOpType
ACTF = mybir.ActivationFunctionType
AX = mybir.AxisListType

D = 1024
L = 2
DFF = 2816
NFC = 22
EPS = 1e-6
NEG = -30000.0
SLOPES = [2.0 ** (-8.0 * (i + 1.0) / 8.0) for i in range(8)]


class Op:
    __slots__ = ("eng", "fn", "deps", "sig", "val", "dkey", "is_dma")

    def __init__(self, eng, fn, deps, dkey=None):
        self.eng = eng
        self.fn = fn
        self.deps = deps
        self.sig = False
        self.val = 0
        self.dkey = dkey
        self.is_dma = dkey is not None


class _Rec:
    def __init__(self):
        self.call = None

    def __getattr__(self, name):
        def f(*args, **kwargs):
            self.call = (name, args, kwargs)
            return None
        return f


def _bind(fn):
    r = _Rec()
    fn(r)
    name, args, kwargs = r.call
    return lambda eng: getattr(eng, name)(*args, **kwargs)


class Res:
    __slots__ = ("w", "r", "pw", "pr", "open", "excl")

    BAR = [None]

    def __init__(self):
        self.w = [Res.BAR[0]] if Res.BAR[0] is not None else []
        self.r = []
        self.pw = []
        self.pr = []
        self.open = False
        self.excl = False


class _Stop(Exception):
    pass


import os
KSTOP = int(os.environ.get('KSTOP', '0'))


def ckpt(n):
    if KSTOP == n:
        raise _Stop()


class Prog:
    ENGS = ("sync", "act", "dve", "pool", "pe")

    def __init__(self, nc):
        self.nc = nc
        self.q = {e: [] for e in self.ENGS}
        self.dma_keys = {}
        self.dma_rr = {}
        self.dma_last = {}

    def _deps(self, eng, reads, writes, pwrites, extra):
        deps = list(extra)
        for R in reads:
            deps += R.w
            if R.excl:
                deps += [r for r in R.r if r.eng != eng]
        for R in writes:
            R.pw, R.pr = R.w, R.r
            R.w, R.r = [], []
            R.open = False
            deps += R.pw + R.pr
        for R in pwrites:
            if R.r or not R.open:
                R.pw, R.pr = R.w, R.r
                R.w, R.r = [], []
                R.open = True
            deps += R.pw + R.pr
        return [d for d in deps if d is not None]

    def _post(self, o, reads, writes, pwrites):
        for R in reads:
            R.r.append(o)
            R.open = False
        for R in writes:
            R.w.append(o)
        for R in pwrites:
            R.w.append(o)

    def op(self, eng, fn, reads=(), writes=(), pwrites=(), extra=()):
        deps = self._deps(eng, reads, writes, pwrites, extra)
        if eng == "pe":
            deps = [d for d in deps if d.is_dma or d.eng != "pe"]
        o = Op(eng, _bind(fn), deps)
        self.q[eng].append(o)
        self._post(o, reads, writes, pwrites)
        return o

    def dma(self, eng, key, fn, reads=(), writes=(), pwrites=(), extra=()):
        if key == "c":
            i = self.dma_rr.get("c", 0)
            self.dma_rr["c"] = i + 1
            key = "c%d" % (i % 8)
        deps = self._deps(eng, reads, writes, pwrites, extra)
        if key in self.dma_last:
            deps.append(self.dma_last[key])
        o = Op(eng, _bind(fn), deps, dkey=key)
        self.dma_keys[key] = self.dma_keys.get(key, 0) + 16
        o.val = self.dma_keys[key]
        self.q[eng].append(o)
        self.dma_last[key] = o
        self._post(o, reads, writes, pwrites)
        return o

    def emit(self, final_waits=()):
        nc = self.nc
        for e in self.ENGS:
            for o in self.q[e]:
                for d in o.deps:
                    if not d.is_dma:
                        d.sig = True
        for e in self.ENGS:
            c = 0
            for o in self.q[e]:
                if (not o.is_dma) and o.sig:
                    c += 1
                    o.val = c
        with contextlib.ExitStack() as es:
            esem = {e: es.enter_context(nc.semaphore("tl_" + e)) for e in self.ENGS}
            dsem = {k: es.enter_context(nc.semaphore("dq_%d" % i)) for i, k in enumerate(self.dma_keys)}
            block = es.enter_context(nc.Block())
            final_waits = list(final_waits)

            def run(ename):
                def body(eng):
                    waited = {}

                    def waits(deps):
                        need = {}
                        for d in deps:
                            key = ("d", d.dkey) if d.is_dma else ("e", d.eng)
                            if need.get(key, 0) < d.val:
                                need[key] = d.val
                        for key, v in need.items():
                            if waited.get(key, 0) < v:
                                sem = dsem[key[1]] if key[0] == "d" else esem[key[1]]
                                eng.wait_ge(sem, v)
                                waited[key] = v

                    for o in self.q[ename]:
                        waits(o.deps)
                        ins = o.fn(eng)
                        if o.is_dma:
                            ins.then_inc(dsem[o.dkey], 16)
                        elif o.sig:
                            ins.then_inc(esem[ename], 1)
                    if ename == "sync":
                        waits(final_waits)
                        if KSTOP != 0:
                            waits(list(self.dma_last.values()))
                return body

            block.sync(run("sync"))
            block.scalar(run("act"))
            block.vector(run("dve"))
            block.gpsimd(run("pool"))
            block.tensor(run("pe"))


def make_consts(smax):
    bf = ml_dtypes.bfloat16
    nt = smax // 128
    c = {}
    c["ident_bf"] = np.eye(128, dtype=np.float32).astype(bf)
    c["ident_f"] = np.eye(128, dtype=np.float32)
    inv = (10000.0 ** (-np.arange(16, dtype=np.float32) * 2.0 / 32.0)).astype(np.float32)
    pos = np.arange(smax, dtype=np.float32)

    def tab(p):
        ang = (p[:, None] * inv[None, :]).astype(np.float32)
        cs, sn = np.cos(ang).astype(np.float32), np.sin(ang).astype(np.float32)
        t = np.concatenate([cs, -sn, sn], axis=1)
        return np.ascontiguousarray(t.reshape(nt, 128, 48).transpose(1, 0, 2))

    c["rope1d"] = tab(pos)
    c["roperow"] = tab(np.floor(pos / 64.0).astype(np.float32))
    c["ropecol"] = tab(np.mod(pos, 64.0).astype(np.float32))
    ki = np.arange(128)[:, None]
    u = np.arange(384)[None, :]
    dist = np.abs((u - 128) - ki).astype(np.float32)
    sa = np.stack([np.where(dist <= 128, -SLOPES[h] * dist, NEG) for h in range(4)]).astype(np.float32)
    c["stripA"] = np.ascontiguousarray(sa.transpose(1, 0, 2)).astype(bf)
    qi = np.arange(128)[None, :]
    dd = np.abs(qi - ki).astype(np.float32)
    td = np.stack([-SLOPES[4 + h] * dd for h in range(4)]).astype(np.float32)
    c["toepD"] = np.ascontiguousarray(td.transpose(1, 0, 2)).astype(bf)
    kp = np.arange(smax)
    c["kaugD"] = np.stack([np.ones(smax), np.ones(smax), 128.0 * (kp // 128), (kp % 128)]).astype(np.float32).astype(bf)
    qh = (kp // 256).astype(np.float32)
    ql = (kp % 256).astype(np.float32)
    qa = np.zeros((4, 4, 2, 2, smax), np.float32)
    for h in range(4):
        m = SLOPES[4 + h]
        bef = np.stack([-m * 256.0 * qh, -m * ql, m * np.ones(smax), m * np.ones(smax)])
        for cc in range(2):
            qa[:, h, cc, 0, :] = bef
            qa[:, h, cc, 1, :] = -bef
    c["qaugD"] = qa.astype(bf)
    return c


CONST_SHAPES = None


def build_program(S_list, nlayers=L):
    nc = bass.Bass("TRN2", target_bir_lowering=False)
    TT = sum(S_list)
    SMAX = max(S_list)
    NTMAX = SMAX // 128
    P = Prog(nc)
    Res.BAR[0] = None

    def din(name, shape, dt=F32):
        return nc.dram_tensor(name, list(shape), dt, kind="ExternalInput")

    x_in = din("x", [TT, D])
    y_out = nc.dram_tensor("y", [TT, D], F32, kind="ExternalOutput")
    xmid = nc.dram_tensor("xmid", [TT, D], F32)
    x1 = nc.dram_tensor("x1", [TT, D], F32)
    w_in = din("w_in", [L, D, 2208])
    w_out = din("w_out", [L, D, D])
    w_up = din("w_up", [L, D, 2 * DFF])
    w_down = din("w_down", [L, DFF, D])
    w_qup = din("b_w_q_up", [L, 256, 384])
    w_kvup = din("b_w_kv_up", [L, 128, 512])
    g_attn_p = din("g_attn_p", [L, 128, 8])
    g_ffn_p = din("g_ffn_p", [L, 128, 8])
    bqn_p = din("bqn_p", [L, 128, 2])
    bkvn_p = din("bkvn_p", [L, 128, 1])
    b_up_p = din("b_up_p", [L, 128, 44])
    conv_w_p = din("conv_w_p", [L, 128, 3 * 44])
    conv_b_p = din("conv_b_p", [L, 128, 44])
    a_sink = din("a_sink", [L, 4])
    cqn = din("c_q_norm", [L, 64])
    ckn = din("c_k_norm", [L, 64])
    dsn = din("d_sub_norm", [L, 64])
    lamv = din("lamv", [L, 128])
    g_final = din("g_final", [1, D])
    c_ident_bf = din("ident_bf", [128, 128], BF16)
    c_ident_f = din("ident_f", [128, 128])
    c_rope1d = din("rope1d", [128, NTMAX, 48])
    c_roperow = din("roperow", [128, NTMAX, 48])
    c_ropecol = din("ropecol", [128, NTMAX, 48])
    c_stripA = din("stripA", [128, 4, 384], BF16)
    c_toepD = din("toepD", [128, 4, 128], BF16)
    c_kaugD = din("kaugD", [4, SMAX], BF16)
    c_qaugD = din("qaugD", [4, 4, 2, 2, SMAX], BF16)

    es = contextlib.ExitStack()
    with es:
        ARENA_F32 = 52800
        arena = es.enter_context(nc.sbuf_tensor("arena", [128, ARENA_F32], F32))
        arena_bf = arena.bitcast(BF16)
        psum = es.enter_context(nc.psum_tensor("psum", [128, 4096], F32))
        psum_bf = psum.bitcast(BF16)

        class Alloc:
            def __init__(self, base=0):
                self.off = base

            def take(self, dt, shape, p0=0, p1=128):
                n = int(np.prod(shape))
                esz = 4 if dt == F32 else 2
                self.off = (self.off + 31) // 32 * 32
                o = self.off // esz
                self.last = self.off
                self.off += n * esz
                assert self.off <= ARENA_F32 * 4, ("SBUF arena overflow", self.off)
                h = arena if dt == F32 else arena_bf
                ap = h[p0:p1, o:o + n]
                return view(ap, shape)

        def view(ap, shape):
            if len(shape) == 1:
                return ap
            names = "abcde"[:len(shape)]
            kw = {names[i]: int(shape[i]) for i in range(len(shape) - 1)}
            return ap.rearrange("p (%s) -> p %s" % (" ".join(names), " ".join(names)), **kw)

        def pview(bank, dt, shape, p0=0, p1=128, nbanks=1):
            n = int(np.prod(shape))
            if dt == F32:
                ap = psum[p0:p1, bank * 512: bank * 512 + n]
            else:
                ap = psum_bf[p0:p1, bank * 1024: bank * 1024 + n]
            return view(ap, shape)

        A0 = Alloc(0)
        ident_bf = A0.take(BF16, [128])
        ident_f = A0.take(F32, [128])
        epsc = A0.take(F32, [1])
        vecs = A0.take(F32, [16])
        R_consts = Res()
        P.dma("sync", "c", lambda e: e.dma_start(out=ident_bf, in_=c_ident_bf.ap()), pwrites=[R_consts])
        P.dma("sync", "c", lambda e: e.dma_start(out=ident_f, in_=c_ident_f.ap()), pwrites=[R_consts])
        P.op("pool", lambda e: e.memset(epsc, EPS), pwrites=[R_consts])
        PERS_END = A0.off

        R_S = [Res() for _ in range(4)]
        R_O = [Res() for _ in range(2)]
        R_T = Res()
        R_Y = Res()
        for _r in R_S + R_O + [R_T, R_Y]:
            _r.excl = True

        prev_tail = [[]]

        def barrier(tail):
            o = P.op("pool", lambda e: e.memset(vecs[:, 0:1], 0.0), writes=list(tail))
            Res.BAR[0] = o

        stores = []
        R_hbm = {}

        def hres(t, row0):
            return R_hbm.setdefault((t.name, row0), Res())

        class WLoader:
            def __init__(self, al, dma_eng="sync", cast_engs=("dve",), key="w", slots=None):
                self.st = slots if slots is not None else [al.take(F32, [512]) for _ in range(2)]
                self.R = [Res() for _ in self.st]
                self.i = 0
                self.dma_eng, self.cast_engs, self.key = dma_eng, cast_engs, key

            def load(self, dst, src, ncols, scale=None, Rdst=None, Rscale=None):
                s = self.i % len(self.st)
                self.i += 1
                st = self.st[s][:, 0:ncols]
                if len(dst.shape) == 3:
                    st = st.rearrange("p (a b) -> p a b", a=dst.shape[1])
                    assert len(src.shape) == 3
                P.dma(self.dma_eng, "%s%d" % (self.key, s), lambda e: e.dma_start(out=st, in_=src), writes=[self.R[s]])
                eng = self.cast_engs[self.i % len(self.cast_engs)]
                rd = [self.R[s]] + ([Rscale] if Rscale is not None else [])
                if scale is None and eng == "act":
                    P.op(eng, lambda e: e.copy(out=dst, in_=st), reads=rd, pwrites=[Rdst])
                elif scale is None:
                    P.op(eng, lambda e: e.tensor_copy(out=dst, in_=st), reads=rd, pwrites=[Rdst])
                elif eng == "pool":
                    P.op(eng, lambda e: e.tensor_scalar(out=dst, in0=st, scalar1=scale, scalar2=0.0, op0=ALU.mult, op1=ALU.add),
                         reads=rd, pwrites=[Rdst])
                elif eng == "act":
                    P.op(eng, lambda e: e.activation(out=dst, in_=st, func=ACTF.Identity, scale=scale),
                         reads=rd, pwrites=[Rdst])
                else:
                    P.op(eng, lambda e: e.tensor_scalar(out=dst, in0=st, scalar1=scale, scalar2=None, op0=ALU.mult),
                         reads=rd, pwrites=[Rdst])

        class Front:
            def __init__(self, al, nx=2):
                self.xt = [al.take(F32, [D]) for _ in range(nx)]
                self.Rx = [Res() for _ in range(nx)]
                self.xn = al.take(BF16, [D])
                self.Rxn = Res()
                self.junk = self.xn
                self.Rjunk = self.Rxn
                self.xnT = al.take(BF16, [8, 128])
                self.RxnT = Res()
                self.st = al.take(F32, [4])
                self.Rst = Res()
                self.i = 0

            def load(self, src_t, row0, nrows=128):
                s = self.i % len(self.xt)
                self.i += 1
                xt = self.xt[s]
                R = self.Rx[s]
                rr = [hres(src_t, r) for r in range((row0 // 128) * 128, row0 + nrows, 128)] if src_t is not x_in else []
                if nrows < 128:
                    P.op("pool", lambda e: e.memset(xt, 0.0), writes=[R])
                    P.dma("sync", "xt%d" % s, lambda e: e.dma_start(out=xt[0:nrows, :], in_=src_t.ap()[row0:row0 + nrows, :]),
                          reads=rr, pwrites=[R])
                else:
                    P.dma("sync", "xt%d" % s, lambda e: e.dma_start(out=xt, in_=src_t.ap()[row0:row0 + nrows, :]),
                          reads=rr, writes=[R])
                return xt, R

            def norm_T(self, xt, Rx, act_ok=True):
                self.norm_a(xt, Rx)
                self.norm_b()

            def norm_a(self, xt, Rx):
                st = self.st
                P.op("act", lambda e: e.activation(out=self.junk, in_=xt, func=ACTF.Square, accum_out=st[:, 0:1]),
                     reads=[Rx], writes=[self.Rxn, self.Rst])
                P.op("act", lambda e: e.activation(out=st[:, 1:2], in_=st[:, 0:1], func=ACTF.Ln, scale=1.0 / D, bias=epsc),
                     reads=[self.Rst, R_consts], pwrites=[self.Rst])
                P.op("act", lambda e: e.activation(out=st[:, 2:3], in_=st[:, 1:2], func=ACTF.Exp, scale=-0.5),
                     reads=[self.Rst], pwrites=[self.Rst])
                P.op("dve", lambda e: e.tensor_scalar(out=self.xn, in0=xt, scalar1=st[:, 2:3], scalar2=None, op0=ALU.mult),
                     reads=[Rx, self.Rst], writes=[self.Rxn])

            def norm_b(self):
                pT = pview(6, BF16, [8, 128])
                for c in range(8):
                    P.op("pe", lambda e, c=c: e.transpose(out=pT[:, c, :], in_=self.xn[:, c * 128:(c + 1) * 128], identity=ident_bf),
                         reads=[self.Rxn, R_consts], pwrites=[R_T])
                P.op("dve", lambda e: e.tensor_copy(out=self.xnT, in_=pT), reads=[R_T], writes=[self.RxnT])

            def final_norm_store(self, xt, Rx, gfin, Rg, row0, nrows, skey, junk, Rjunk):
                st = self.st
                P.op("act", lambda e: e.activation(out=junk, in_=xt, func=ACTF.Square, accum_out=st[:, 0:1]),
                     reads=[Rx], writes=[Rjunk, self.Rst])
                P.op("act", lambda e: e.activation(out=st[:, 1:2], in_=st[:, 0:1], func=ACTF.Ln, scale=1.0 / D, bias=epsc),
                     reads=[self.Rst, R_consts], pwrites=[self.Rst])
                P.op("act", lambda e: e.activation(out=st[:, 2:3], in_=st[:, 1:2], func=ACTF.Exp, scale=-0.5),
                     reads=[self.Rst], pwrites=[self.Rst])
                P.op("dve", lambda e: e.scalar_tensor_tensor(out=xt, in0=xt, scalar=st[:, 2:3], in1=gfin, op0=ALU.mult, op1=ALU.mult),
                     reads=[self.Rst, Rg, Rx], pwrites=[Rx])
                stores.append(P.dma("pool", skey, lambda e: e.dma_start(out=y_out.ap()[row0:row0 + nrows, :], in_=xt[0:nrows, :]),
                                    reads=[Rx]))

        def rstd_small(src, dst, tmp, n, Rs, width):
            P.op("act", lambda e: e.activation(out=tmp, in_=src, func=ACTF.Ln, scale=1.0 / width, bias=epsc),
                 reads=[Rs, R_consts], pwrites=[Rs])
            P.op("act", lambda e: e.activation(out=dst, in_=tmp, func=ACTF.Exp, scale=-0.5), reads=[Rs], pwrites=[Rs])

        def rope_seg(xin, xout, tabs, H, Rin, Rout, Rtab, tmp, Rtmp):
            cosb = tabs[:, 0:16].unsqueeze(1).unsqueeze(1).to_broadcast([128, H, 2, 16])
            nsin = tabs[:, 16:32].unsqueeze(1).to_broadcast([128, H, 16])
            psin = tabs[:, 32:48].unsqueeze(1).to_broadcast([128, H, 16])
            P.op("dve", lambda e: e.tensor_tensor(out=tmp[:, :, 0, :], in0=xin[:, :, 1, :], in1=nsin, op=ALU.mult),
                 reads=[Rin, Rtab], writes=[Rtmp])
            P.op("dve", lambda e: e.tensor_tensor(out=tmp[:, :, 1, :], in0=xin[:, :, 0, :], in1=psin, op=ALU.mult),
                 reads=[Rin, Rtab], pwrites=[Rtmp])
            P.op("dve", lambda e: e.tensor_tensor(out=xout, in0=xin, in1=cosb, op=ALU.mult), reads=[Rin, Rtab], pwrites=[Rout])
            P.op("dve", lambda e: e.tensor_tensor(out=xout, in0=xout, in1=tmp, op=ALU.add), reads=[Rtmp, Rout], pwrites=[Rout])

        def build_body():
            for si, S in enumerate(S_list):
                row_base = sum(S_list[:si])
                NT = S // 128
                NQB = S // 512
                for l in range(nlayers):
                    src_t = x_in if l == 0 else x1
                    dst_t = x1 if l < nlayers - 1 else None
                    lam_init = 0.8 - 0.6 * math.exp(-0.3 * l)
                    for grp in range(2):
                        barrier(prev_tail[0])
                        al = Alloc(PERS_END)
                        if grp == 0:
                            KT1 = al.take(BF16, [NT * 128])
                            KT2 = al.take(BF16, [4, NT * 128], 0, 96)
                            nv = 6
                        else:
                            KT1 = al.take(BF16, [NT * 128])
                            KT2 = al.take(BF16, [4, NT * 128], 0, 68)
                            nv = 6
                        Vst = al.take(BF16, [NT, nv, 65])
                        R_KT1, R_KT2, R_V = Res(), Res(), Res()
                        rope_a = al.take(F32, [NT, 48])
                        rope_b = al.take(F32, [48]) if grp == 1 else None
                        R_tab = Res()
                        gq = al.take(F32, [64])
                        gk = al.take(F32, [64])
                        gsub = al.take(F32, [64])
                        lamt = al.take(F32, [8])
                        lraw = al.take(F32, [128])
                        esink = al.take(F32, [4])
                        gcol = al.take(F32, [12])
                        bias_tab = al.take(BF16, [4, 384]) if grp == 0 else al.take(BF16, [4, 128])
                        R_sm = Res()
                        if grp == 0:
                            P.dma("sync", "c", lambda e: e.dma_start(out=rope_a, in_=c_rope1d.ap()[:, 0:NT, :]), pwrites=[R_tab])
                            P.dma("sync", "c", lambda e: e.dma_start(out=bias_tab, in_=c_stripA.ap()), pwrites=[R_tab])
                            P.dma("sync", "c", lambda e: e.dma_start(out=esink, in_=a_sink.ap()[l:l + 1, :].partition_broadcast(128)), writes=[R_sm])
                            P.op("act", lambda e: e.activation(out=esink, in_=esink, func=ACTF.Exp), reads=[R_sm], pwrites=[R_sm])
                        else:
                            P.dma("sync", "c", lambda e: e.dma_start(out=rope_a, in_=c_roperow.ap()[:, 0:NT, :]), pwrites=[R_tab])
                            P.dma("sync", "c", lambda e: e.dma_start(out=rope_b, in_=c_ropecol.ap()[:, 0, :]), pwrites=[R_tab])
                            P.dma("sync", "c", lambda e: e.dma_start(out=bias_tab, in_=c_toepD.ap()), pwrites=[R_tab])
                            P.dma("sync", "c", lambda e: e.dma_start(out=gq, in_=cqn.ap()[l:l + 1, :].partition_broadcast(128)), pwrites=[R_sm])
                            P.dma("sync", "c", lambda e: e.dma_start(out=gk, in_=ckn.ap()[l:l + 1, :].partition_broadcast(128)), pwrites=[R_sm])
                            P.dma("sync", "c", lambda e: e.dma_start(out=gsub, in_=dsn.ap()[l:l + 1, :].partition_broadcast(128)), pwrites=[R_sm])
                            P.dma("sync", "c", lambda e: e.dma_start(out=lraw, in_=lamv.ap()[l:l + 1, :].partition_broadcast(128)), pwrites=[R_sm])
                            for h in range(4):
                                P.dma("sync", "c", lambda e, h=h: e.dma_start(out=KT2[64:68, h, :], in_=c_kaugD.ap()[:, 0:S]), pwrites=[R_KT2])
                            lr = lraw.rearrange("p (a b) -> p a b", a=4)
                            P.op("dve", lambda e: e.tensor_tensor(out=lr[:, 0, :], in0=lr[:, 0, :], in1=lr[:, 1, :], op=ALU.mult), reads=[R_sm], pwrites=[R_sm])
                            P.op("dve", lambda e: e.tensor_tensor(out=lr[:, 2, :], in0=lr[:, 2, :], in1=lr[:, 3, :], op=ALU.mult), reads=[R_sm], pwrites=[R_sm])
                            P.op("dve", lambda e: e.tensor_reduce(out=lamt[:, 0:1], in_=lr[:, 0, :], axis=AX.X, op=ALU.add), reads=[R_sm], pwrites=[R_sm])
                            P.op("dve", lambda e: e.tensor_reduce(out=lamt[:, 1:2], in_=lr[:, 2, :], axis=AX.X, op=ALU.add), reads=[R_sm], pwrites=[R_sm])
                            P.op("act", lambda e: e.activation(out=lamt[:, 2:4], in_=lamt[:, 0:2], func=ACTF.Exp), reads=[R_sm], pwrites=[R_sm])
                            P.op("dve", lambda e: e.tensor_tensor(out=lamt[:, 4:5], in0=lamt[:, 3:4], in1=lamt[:, 2:3], op=ALU.subtract), reads=[R_sm], pwrites=[R_sm])
                            P.op("dve", lambda e: e.tensor_scalar(out=lamt[:, 5:6], in0=lamt[:, 4:5], scalar1=-lam_init, scalar2=None, op0=ALU.add), reads=[R_sm], pwrites=[R_sm])
                            P.op("dve", lambda e: e.tensor_scalar(out=gsub, in0=gsub, scalar1=(1.0 - lam_init), scalar2=None, op0=ALU.mult), reads=[R_sm], pwrites=[R_sm])
                        neglam = lamt[:, 5:6]
                        P.dma("sync", "c", lambda e: e.dma_start(out=gcol[:, 0:8], in_=g_attn_p.ap()[l]), pwrites=[R_sm])
                        P.dma("sync", "c", lambda e: e.dma_start(out=gcol[:, 8:10], in_=bqn_p.ap()[l]), pwrites=[R_sm])
                        P.dma("sync", "c", lambda e: e.dma_start(out=gcol[:, 10:11], in_=bkvn_p.ap()[l]), pwrites=[R_sm])
                        P.op("dve", lambda e: e.memset(Vst[:, :, :, 64:65], 1.0), pwrites=[R_V])
                        GRP_END = al.off
                        ckpt(11)

                        aw = Alloc(GRP_END)
                        wlq = WLoader(aw, dma_eng="pool", cast_engs=("pool",), key="wp")
                        wq = aw.take(BF16, [8, 512])
                        wo = aw.take(BF16, [4, D])
                        wqup = aw.take(BF16, [2, 384]) if grp == 0 else None
                        R_wq, R_wo = Res(), Res()
                        W_END = aw.off
                        a1 = Alloc(W_END)
                        wl = WLoader(a1)
                        ncol_kv = 416 if grp == 0 else 768
                        wkv = a1.take(BF16, [8, ncol_kv])
                        R_wkv = Res()
                        if grp == 0:
                            kvsrc = [(1024 - 768, 0, 128), (1024 - 768 + 128, 128, 128)]
                        if grp == 0:
                            kvsrc = [(256, 0, 256), (768, 256, 160)]
                        else:
                            kvsrc = [(1184, 0, 256), (1696, 256, 512)]
                        for (sc, dc, n) in kvsrc:
                            for k in range(8):
                                wl.load(wkv[:, k, dc:dc + n], w_in.ap()[l, k * 128:(k + 1) * 128, sc:sc + n], n,
                                        scale=gcol[:, k:k + 1], Rdst=R_wkv, Rscale=R_sm)
                        if grp == 0:
                            wkvup = a1.take(BF16, [512])
                            wl.load(wkvup, w_kvup.ap()[l], 512, scale=gcol[:, 10:11], Rdst=R_wkv, Rscale=R_sm)
                        ckpt(12)
                        qsrc = [(0, 0, 256), (512, 256, 256)] if grp == 0 else [(928, 0, 256), (1440, 256, 256)]
                        for (sc, dc, n) in qsrc:
                            for k in range(8):
                                wlq.load(wq[:, k, dc:dc + n], w_in.ap()[l, k * 128:(k + 1) * 128, sc:sc + n], n,
                                         scale=gcol[:, k:k + 1], Rdst=R_wq, Rscale=R_sm)
                        for k in range(4):
                            for hf in range(2):
                                wlq.load(wo[:, k, hf * 512:(hf + 1) * 512],
                                         w_out.ap()[l, grp * 512 + k * 128: grp * 512 + (k + 1) * 128, hf * 512:(hf + 1) * 512], 512, Rdst=R_wo)
                        if grp == 0:
                            for k in range(2):
                                wlq.load(wqup[:, k, :], w_qup.ap()[l, k * 128:(k + 1) * 128, :], 384, scale=gcol[:, 8 + k:9 + k], Rdst=R_wq, Rscale=R_sm)
                        frs = [Front(a1, nx=1) for _ in range(2)]
                        stg_f2 = [a1.take(F32, [768]) for _ in range(2)]
                        stg_b2 = [a1.take(BF16, [512]) for _ in range(2)]
                        tmp_f2 = [a1.take(F32, [256]) for _ in range(2)]
                        sst_2 = [a1.take(F32, [16]) for _ in range(2)]
                        ckvT2 = [a1.take(BF16, [128]) for _ in range(2)]
                        R_stg2 = [Res(), Res()]
                        R_stgb2 = [Res(), Res()]
                        R_tmp2 = [Res(), Res()]
                        R_sst_2 = [Res(), Res()]
                        R_ckvT2 = [Res(), Res()]

                        def p1_gen(t, par):
                            fr = frs[par]
                            stg_f, stg_b, tmp_f, sst, ckvT = stg_f2[par], stg_b2[par], tmp_f2[par], sst_2[par], ckvT2[par]
                            R_stg, R_stgb, R_tmp, R_sst, R_ckvT = R_stg2[par], R_stgb2[par], R_tmp2[par], R_sst_2[par], R_ckvT2[par]
                            b0 = 2 * par
                            RS0, RS1 = R_S[b0], R_S[b0 + 1]
                            xt, Rx = fr.load(src_t, row_base + t * 128)
                            yield
                            fr.norm_a(xt, Rx)
                            yield
                            fr.norm_b()
                            yield
                            pk = pview(b0, F32, [ncol_kv])
                            for (c0, c1, bk) in ([(0, 416, 0)] if grp == 0 else [(0, 512, 0), (512, 768, 1)]):
                                for k in range(8):
                                    P.op("pe", lambda e: e.matmul(pk[:, c0:c1], lhsT=fr.xnT[:, k, :], rhs=wkv[:, k, c0:c1], start=(k == 0), stop=(k == 7)),
                                         reads=[fr.RxnT, R_wkv], pwrites=[R_S[b0 + bk]])
                            yield
                            tcol = slice(t * 128, (t + 1) * 128)
                            pT = pview(6, BF16, [8, 128])
                            if grp == 0:
                                P.op("act", lambda e: e.copy(out=stg_b[:, 0:128], in_=pk[:, 0:128]), reads=[RS0], writes=[R_stgb])
                                P.op("act", lambda e: e.copy(out=Vst[:, t, 0:2, 0:64], in_=pk[:, 128:256].rearrange("p (a b) -> p a b", a=2)),
                                     reads=[RS0], pwrites=[R_V])
                                P.op("dve", lambda e: e.tensor_copy(out=stg_f[:, 0:160], in_=pk[:, 256:416]), reads=[RS0], writes=[R_stg])
                                P.op("dve", lambda e: e.tensor_tensor(out=tmp_f[:, 0:128], in0=stg_f[:, 0:128], in1=stg_f[:, 0:128], op=ALU.mult),
                                     reads=[R_stg], writes=[R_tmp])
                                P.op("dve", lambda e: e.tensor_reduce(out=sst[:, 0:1], in_=tmp_f[:, 0:128], axis=AX.X, op=ALU.add),
                                     reads=[R_tmp], writes=[R_sst])
                                yield
                                P.op("pe", lambda e: e.transpose(out=pT[:, 0, :], in_=stg_b[:, 0:128], identity=ident_bf),
                                     reads=[R_stgb, R_consts], writes=[R_T])
                                P.op("dve", lambda e: e.tensor_copy(out=KT1[:, tcol], in_=pT[:, 0, :]), reads=[R_T], pwrites=[R_KT1])
                                rstd_small(sst[:, 0:1], sst[:, 2:3], sst[:, 1:2], 1, R_sst, 128.0)
                                P.op("dve", lambda e: e.tensor_scalar(out=stg_b[:, 128:256], in0=stg_f[:, 0:128], scalar1=sst[:, 2:3], scalar2=None, op0=ALU.mult),
                                     reads=[R_stg, R_sst], pwrites=[R_stgb])
                                yield
                                P.op("pe", lambda e: e.transpose(out=pT[:, 1, :], in_=stg_b[:, 128:256], identity=ident_bf),
                                     reads=[R_stgb, R_consts], writes=[R_T])
                                P.op("dve", lambda e: e.tensor_copy(out=ckvT, in_=pT[:, 1, :]), reads=[R_T], writes=[R_ckvT])
                                yield
                                pkv = pview(7, F32, [512])
                                P.op("pe", lambda e: e.matmul(pkv, lhsT=ckvT, rhs=wkvup, start=True, stop=True), reads=[R_ckvT, R_wkv], writes=[R_Y])
                                kcat = stg_b[:, 0:384].rearrange("p (a b) -> p a b", a=4)
                                pkv4 = pkv.rearrange("p (a b) -> p a b", a=4)
                                P.op("dve", lambda e: e.tensor_copy(out=kcat[:, :, 0:64], in_=pkv4[:, :, 0:64]), reads=[R_Y], writes=[R_stgb])
                                P.op("act", lambda e: e.copy(out=Vst[:, t, 2:6, 0:64], in_=pkv4[:, :, 64:128]), reads=[R_Y], pwrites=[R_V])
                                kr_in = stg_f[:, 128:160].rearrange("p (h a b) -> p h a b", h=1, a=2)
                                kr_out = tmp_f[:, 64:96].rearrange("p (h a b) -> p h a b", h=1, a=2)
                                kr_tmp = tmp_f[:, 128:160].rearrange("p (h a b) -> p h a b", h=1, a=2)
                                rope_seg(kr_in, kr_out, rope_a[:, t, :], 1, R_stg, R_tmp, R_tab, kr_tmp, R_tmp)
                                P.op("dve", lambda e: e.tensor_copy(out=kcat[:, :, 64:96], in_=tmp_f[:, 64:96].unsqueeze(1).to_broadcast([128, 4, 32])),
                                     reads=[R_tmp], pwrites=[R_stgb])
                                yield
                                for h in range(4):
                                    P.op("pe", lambda e: e.transpose(out=pT[0:96, 2 + h, :], in_=kcat[:, h, :], identity=ident_bf),
                                         reads=[R_stgb, R_consts], pwrites=[R_T])
                                P.op("dve", lambda e: e.tensor_copy(out=KT2[:, :, tcol], in_=pT[0:96, 2:6, :]), reads=[R_T], pwrites=[R_KT2])
                            else:
                                P.op("dve", lambda e: e.tensor_copy(out=stg_f[:, 0:128], in_=pk[:, 0:128]), reads=[RS0], writes=[R_stg])
                                P.op("act", lambda e: e.copy(out=Vst[:, t, 0:2, 0:64], in_=pk[:, 128:256].rearrange("p (a b) -> p a b", a=2)),
                                     reads=[RS0], pwrites=[R_V])
                                ck = stg_f[:, 0:128].rearrange("p (a b) -> p a b", a=2)
                                P.op("dve", lambda e: e.tensor_tensor(out=tmp_f[:, 0:128], in0=stg_f[:, 0:128], in1=stg_f[:, 0:128], op=ALU.mult),
                                     reads=[R_stg], writes=[R_tmp])
                                P.op("dve", lambda e: e.tensor_reduce(out=sst[:, 0:2], in_=tmp_f[:, 0:128].rearrange("p (a b) -> p a b", a=2), axis=AX.X, op=ALU.add),
                                     reads=[R_tmp], writes=[R_sst])
                                P.op("act", lambda e: e.copy(out=stg_b[:, 128:384], in_=pk[:, 256:512]), reads=[RS0], writes=[R_stgb])
                                P.op("act", lambda e: e.copy(out=Vst[:, t, 2:6, 0:64], in_=pk[:, 512:768].rearrange("p (a b) -> p a b", a=4)),
                                     reads=[RS1], pwrites=[R_V])
                                yield
                                for h in range(4):
                                    P.op("pe", lambda e: e.transpose(out=pT[0:64, 2 + h, :], in_=stg_b[:, 128 + 64 * h:192 + 64 * h], identity=ident_bf),
                                         reads=[R_stgb, R_consts], pwrites=[R_T])
                                P.op("dve", lambda e: e.tensor_copy(out=KT2[0:64, :, tcol], in_=pT[0:64, 2:6, :]), reads=[R_T], pwrites=[R_KT2])
                                rstd_small(sst[:, 0:2], sst[:, 4:6], sst[:, 2:4], 2, R_sst, 64.0)
                                P.op("dve", lambda e: e.tensor_tensor(out=ck, in0=ck, in1=sst[:, 4:6].unsqueeze(2).to_broadcast([128, 2, 64]), op=ALU.mult),
                                     reads=[R_sst, R_stg], pwrites=[R_stg])
                                P.op("dve", lambda e: e.tensor_tensor(out=ck, in0=ck, in1=gk.unsqueeze(1).to_broadcast([128, 2, 64]), op=ALU.mult),
                                     reads=[R_sm, R_stg], pwrites=[R_stg])
                                ck5 = stg_f[:, 0:128].rearrange("p (h s a b) -> p h s a b", h=2, s=2, a=2)
                                co5 = stg_f[:, 256:384].rearrange("p (h s a b) -> p h s a b", h=2, s=2, a=2)
                                tm5 = tmp_f[:, 0:128].rearrange("p (h s a b) -> p h s a b", h=2, s=2, a=2)
                                rope_seg(ck5[:, :, 0], co5[:, :, 0], rope_a[:, t, :], 2, R_stg, R_stg, R_tab, tm5[:, :, 0], R_tmp)
                                rope_seg(ck5[:, :, 1], co5[:, :, 1], rope_b, 2, R_stg, R_stg, R_tab, tm5[:, :, 1], R_tmp)
                                P.op("dve", lambda e: e.tensor_copy(out=stg_b[:, 0:128], in_=stg_f[:, 256:384]), reads=[R_stg], pwrites=[R_stgb])
                                yield
                                P.op("pe", lambda e: e.transpose(out=pT[:, 0, :], in_=stg_b[:, 0:128], identity=ident_bf),
                                     reads=[R_stgb, R_consts], writes=[R_T])
                                P.op("dve", lambda e: e.tensor_copy(out=KT1[:, tcol], in_=pT[:, 0, :]), reads=[R_T], pwrites=[R_KT1])

                        import collections as _co1
                        _g = _co1.deque()
                        _nt = 0
                        while _nt < NT or _g:
                            while len(_g) < 2 and _nt < NT:
                                _g.append(p1_gen(_nt, _nt % 2))
                                _nt += 1
                            g_ = _g.popleft()
                            try:
                                next(g_)
                                _g.append(g_)
                            except StopIteration:
                                pass
                        fr = frs[0]

                        if KSTOP == 1 + 2 * grp:
                            raise _Stop()
                        p1_tail = [R_wkv] + R_stg2 + R_stgb2 + R_tmp2 + R_sst_2 + R_ckvT2 + wl.R
                        for f_ in frs:
                            p1_tail += [f_.RxnT, f_.Rxn, f_.Rst] + f_.Rx
                        barrier(p1_tail)
                        a2 = Alloc(W_END)
                        wl = wlq
                        fr = Front(a2)
                        xres = [a2.take(F32, [D]) for _ in range(2)]
                        R_xres = [Res() for _ in range(2)]
                        stg_f = a2.take(F32, [768])
                        R_stg = Res()
                        stg_b = a2.take(BF16, [768])
                        R_stgb = Res()
                        tmp_f = a2.take(F32, [512])
                        R_tmp = Res()
                        sst = a2.take(F32, [16])
                        R_sst = Res()
                        cqT = a2.take(BF16, [2, 128])
                        R_cqT = Res()
                        if grp == 0:
                            Q1 = [a2.take(BF16, [4, 512]) for _ in range(2)]
                            Q2 = [a2.take(BF16, [4, 512], 0, 96) for _ in range(2)]
                        else:
                            Q1 = [a2.take(BF16, [4, 512]) for _ in range(2)]
                            Q2 = [a2.take(BF16, [4, 2, 2, 512], 0, 68) for _ in range(2)]
                        R_Q1 = [Res(), Res()]
                        R_Q2 = [Res(), Res()]
                        for b_ in range(2):
                            P.op("pool", lambda e, b_=b_: e.memset(Q1[b_], 0.0), writes=[R_Q1[b_]])
                        NPB = 4
                        PT = [a2.take(BF16, [2, 512]) for _ in range(NPB)]
                        R_PT = [Res() for _ in range(NPB)]
                        Osb = [a2.take(F32, [512], 0, 65) for _ in range(2)]
                        R_Osb = [Res() for _ in range(2)]
                        rz = a2.take(F32, [8])
                        R_rz = Res()
                        otm = [a2.take(BF16, [4, 512]) for _ in range(2)]
                        R_otm = [Res(), Res()]
                        od = a2.take(F32, [4, 2, 64])
                        R_od = Res()
                        dtmp = a2.take(F32, [4, 64])
                        R_dtmp = Res()
                        sst2 = a2.take(F32, [16])
                        R_sst2 = Res()
                        oT = a2.take(BF16, [4, 128])
                        R_oT = Res()
                        xo = [a2.take(F32, [D]) for _ in range(2)]
                        R_xo = [Res() for _ in range(2)]
                        if os.environ.get('KDEBUG'):
                            print('P2 alloc end', grp, S, a2.off, 'grp_end', GRP_END)
                        pt_i = [0]
                        o_i = [0]

                        def qprep_gen(qb, j):
                            qi = qb % 2
                            Q1c, Q2c, RQ1, RQ2 = Q1[qi], Q2[qi], R_Q1[qi], R_Q2[qi]
                            t = qb * 4 + j
                            if grp == 1 and j == 0:
                                q0 = qb * 512
                                P.dma("sync", "c", lambda e: e.dma_start(out=Q2c[64:68], in_=c_qaugD.ap()[:, :, :, :, q0:q0 + 512]), pwrites=[RQ2])
                            xt, Rx = fr.load(src_t, row_base + t * 128)
                            yield
                            fr.norm_a(xt, Rx)
                            yield
                            fr.norm_b()
                            yield
                            pq = pview(7, F32, [512])
                            for k in range(8):
                                P.op("pe", lambda e, k=k: e.matmul(pq, lhsT=fr.xnT[:, k, :], rhs=wq[:, k, :], start=(k == 0), stop=(k == 7)),
                                     reads=[fr.RxnT, R_wq], pwrites=[R_Y])
                            yield
                            jc = slice(j * 128, (j + 1) * 128)
                            pT = pview(6, BF16, [8, 128])
                            if grp == 0:
                                aq = stg_b[:, 0:256].rearrange("p (i g d) -> p i g d", i=2, g=2)
                                pqa = pq[:, 0:256].rearrange("p (g i d) -> p i g d", g=2, i=2)
                                P.op("dve", lambda e: e.tensor_scalar(out=aq, in0=pqa, scalar1=0.125, scalar2=None, op0=ALU.mult),
                                     reads=[R_Y], writes=[R_stgb])
                                P.op("dve", lambda e: e.tensor_copy(out=stg_f[:, 0:256], in_=pq[:, 256:512]), reads=[R_Y], writes=[R_stg])
                                P.op("dve", lambda e: e.tensor_tensor(out=tmp_f[:, 0:256], in0=stg_f[:, 0:256], in1=stg_f[:, 0:256], op=ALU.mult),
                                     reads=[R_stg], writes=[R_tmp])
                                P.op("dve", lambda e: e.tensor_reduce(out=sst[:, 0:1], in_=tmp_f[:, 0:256], axis=AX.X, op=ALU.add),
                                     reads=[R_tmp], writes=[R_sst])
                                yield
                                for i in range(2):
                                    P.op("pe", lambda e, i=i: e.transpose(out=pT[:, i, :], in_=stg_b[:, i * 128:(i + 1) * 128], identity=ident_bf),
                                         reads=[R_stgb, R_consts], pwrites=[R_T])
                                P.op("dve", lambda e: e.tensor_copy(out=Q1c[0:64, 0:2, jc], in_=pT[0:64, 0:2, :]), reads=[R_T], pwrites=[RQ1])
                                P.op("dve", lambda e: e.tensor_copy(out=Q1c[64:128, 2:4, jc], in_=pT[64:128, 0:2, :]), reads=[R_T], pwrites=[RQ1])
                                rstd_small(sst[:, 0:1], sst[:, 2:3], sst[:, 1:2], 1, R_sst, 256.0)
                                P.op("dve", lambda e: e.tensor_scalar(out=stg_b[:, 256:512], in0=stg_f[:, 0:256], scalar1=sst[:, 2:3], scalar2=None, op0=ALU.mult),
                                     reads=[R_stg, R_sst], pwrites=[R_stgb])
                                yield
                                for i in range(2):
                                    P.op("pe", lambda e, i=i: e.transpose(out=pT[:, 2 + i, :], in_=stg_b[:, 256 + i * 128:384 + i * 128], identity=ident_bf),
                                         reads=[R_stgb, R_consts], pwrites=[R_T])
                                P.op("dve", lambda e: e.tensor_copy(out=cqT, in_=pT[:, 2:4, :]), reads=[R_T], writes=[R_cqT])
                                yield
                                pqq = pview(7, F32, [384])
                                for i in range(2):
                                    P.op("pe", lambda e, i=i: e.matmul(pqq, lhsT=cqT[:, i, :], rhs=wqup[:, i, :], start=(i == 0), stop=(i == 1)),
                                         reads=[R_cqT, R_wq], pwrites=[R_Y])
                                yield
                                sc_b = 96.0 ** -0.5
                                P.op("dve", lambda e: e.tensor_scalar(out=stg_f[:, 256:640], in0=pqq, scalar1=sc_b, scalar2=None, op0=ALU.mult),
                                     reads=[R_Y], pwrites=[R_stg])
                                q4 = stg_f[:, 256:640].rearrange("p (h d) -> p h d", h=4)
                                qr_in = q4[:, :, 64:96].rearrange("p h (a b) -> p h a b", a=2)
                                qr_out = tmp_f[:, 0:128].rearrange("p (h a b) -> p h a b", h=4, a=2)
                                qr_tmp = tmp_f[:, 128:256].rearrange("p (h a b) -> p h a b", h=4, a=2)
                                rope_seg(qr_in, qr_out, rope_a[:, t, :], 4, R_stg, R_tmp, R_tab, qr_tmp, R_tmp)
                                qcat = stg_b[:, 0:384].rearrange("p (h d) -> p h d", h=4)
                                P.op("dve", lambda e: e.tensor_copy(out=qcat[:, :, 0:64], in_=q4[:, :, 0:64]), reads=[R_stg], writes=[R_stgb])
                                P.op("dve", lambda e: e.tensor_copy(out=qcat[:, :, 64:96], in_=tmp_f[:, 0:128].rearrange("p (h d) -> p h d", h=4)),
                                     reads=[R_tmp], pwrites=[R_stgb])
                                yield
                                for h in range(4):
                                    P.op("pe", lambda e, h=h: e.transpose(out=pT[0:96, 4 + h, :], in_=qcat[:, h, :], identity=ident_bf),
                                         reads=[R_stgb, R_consts], pwrites=[R_T])
                                P.op("dve", lambda e: e.tensor_copy(out=Q2c[:, :, jc], in_=pT[0:96, 4:8, :]), reads=[R_T], pwrites=[RQ2])
                            else:
                                P.op("dve", lambda e: e.tensor_copy(out=stg_f[:, 0:256], in_=pq[:, 0:256]), reads=[R_Y], writes=[R_stg])
                                cq4 = stg_f[:, 0:256].rearrange("p (a b) -> p a b", a=4)
                                P.op("dve", lambda e: e.tensor_tensor(out=tmp_f[:, 0:256], in0=stg_f[:, 0:256], in1=stg_f[:, 0:256], op=ALU.mult),
                                     reads=[R_stg], writes=[R_tmp])
                                P.op("dve", lambda e: e.tensor_reduce(out=sst[:, 0:4], in_=tmp_f[:, 0:256].rearrange("p (a b) -> p a b", a=4), axis=AX.X, op=ALU.add),
                                     reads=[R_tmp], writes=[R_sst])
                                sc_d = 32.0 ** -0.5
                                dq = stg_b[:, 256:768].rearrange("p (h c d) -> p h c d", h=4, c=2)
                                P.op("pool", lambda e: e.memset(stg_b[:, 256:768], 0.0), writes=[R_stgb])
                                pqd = pq[:, 256:512].rearrange("p (h c d) -> p h c d", h=4, c=2)
                                for cc in range(2):
                                    P.op("dve", lambda e, cc=cc: e.tensor_scalar(out=dq[:, :, cc, 32 * cc:32 * cc + 32], in0=pqd[:, :, cc, :], scalar1=sc_d, scalar2=None, op0=ALU.mult),
                                         reads=[R_Y], pwrites=[R_stgb])
                                yield
                                pT8 = pview(6, BF16, [8, 128])
                                for h in range(4):
                                    for cc in range(2):
                                        P.op("pe", lambda e, h=h, cc=cc: e.transpose(out=pT8[0:64, h * 2 + cc, :], in_=dq[:, h, cc, :], identity=ident_bf),
                                             reads=[R_stgb, R_consts], pwrites=[R_T])
                                pT8v = pT8[0:64, :, :].rearrange("p (h c) t -> p h c t", h=4)
                                for v in range(2):
                                    P.op("dve", lambda e, v=v: e.tensor_copy(out=Q2c[0:64, :, :, v, jc], in_=pT8v), reads=[R_T], pwrites=[RQ2])
                                rstd_small(sst[:, 0:4], sst[:, 8:12], sst[:, 4:8], 4, R_sst, 64.0)
                                P.op("dve", lambda e: e.tensor_tensor(out=cq4, in0=cq4, in1=sst[:, 8:12].unsqueeze(2).to_broadcast([128, 4, 64]), op=ALU.mult),
                                     reads=[R_sst, R_stg], pwrites=[R_stg])
                                P.op("dve", lambda e: e.scalar_tensor_tensor(out=stg_f[:, 0:256].rearrange("p (a b) -> p a b", a=4), in0=cq4, scalar=0.125,
                                                                            in1=gq.unsqueeze(1).to_broadcast([128, 4, 64]), op0=ALU.mult, op1=ALU.mult),
                                     reads=[R_sm, R_stg], pwrites=[R_stg])
                                cq5 = stg_f[:, 0:256].rearrange("p (h s a b) -> p h s a b", h=4, s=2, a=2)
                                co5 = stg_f[:, 256:512].rearrange("p (h s a b) -> p h s a b", h=4, s=2, a=2)
                                tm5 = tmp_f[:, 0:256].rearrange("p (h s a b) -> p h s a b", h=4, s=2, a=2)
                                rope_seg(cq5[:, :, 0], co5[:, :, 0], rope_a[:, t, :], 4, R_stg, R_stg, R_tab, tm5[:, :, 0], R_tmp)
                                rope_seg(cq5[:, :, 1], co5[:, :, 1], rope_b, 4, R_stg, R_stg, R_tab, tm5[:, :, 1], R_tmp)
                                P.op("dve", lambda e: e.tensor_copy(out=stg_b[:, 0:256].rearrange("p (i g d) -> p i g d", i=2, g=2),
                                                                   in_=stg_f[:, 256:512].rearrange("p (g i d) -> p i g d", g=2, i=2)),
                                     reads=[R_stg], writes=[R_stgb])
                                yield
                                for i in range(2):
                                    P.op("pe", lambda e, i=i: e.transpose(out=pT[:, i, :], in_=stg_b[:, i * 128:(i + 1) * 128], identity=ident_bf),
                                         reads=[R_stgb, R_consts], pwrites=[R_T])
                                P.op("dve", lambda e: e.tensor_copy(out=Q1c[0:64, 0:2, jc], in_=pT[0:64, 0:2, :]), reads=[R_T], pwrites=[RQ1])
                                P.op("dve", lambda e: e.tensor_copy(out=Q1c[64:128, 2:4, jc], in_=pT[64:128, 0:2, :]), reads=[R_T], pwrites=[RQ1])

                        def outproj_gen(qb, j):
                            t = qb * 4 + j
                            row0 = row_base + t * 128
                            otc, Rotc = otm[qb % 2], R_otm[qb % 2]
                            xr = xres[j % 2]
                            Rxr = R_xres[j % 2]
                            rsrc = src_t if grp == 0 else xmid
                            rr = [hres(rsrc, row0)] if rsrc is not x_in else []
                            P.dma("sync", "xr%d" % (j % 2), lambda e: e.dma_start(out=xr, in_=rsrc.ap()[row0:row0 + 128, :]), reads=rr, writes=[Rxr])
                            yield
                            pOT = pview(6, BF16, [8, 128])
                            for c in range(4):
                                P.op("pe", lambda e, c=c: e.transpose(out=pOT[:, c, :], in_=otc[:, j, c * 128:(c + 1) * 128], identity=ident_bf),
                                     reads=[Rotc, R_consts], pwrites=[R_T])
                            P.op("dve", lambda e: e.tensor_copy(out=oT, in_=pOT[:, 0:4, :]), reads=[R_T], writes=[R_oT])
                            xw = xo[j % 2]
                            Rxw = R_xo[j % 2]
                            for hf in range(2):
                                yield
                                pY = pview(7, F32, [512])
                                for c in range(4):
                                    P.op("pe", lambda e, c=c, hf=hf: e.matmul(pY, lhsT=oT[:, c, :], rhs=wo[:, c, hf * 512:(hf + 1) * 512], start=(c == 0), stop=(c == 3)),
                                         reads=[R_oT, R_wo], pwrites=[R_Y])
                                P.op("dve", lambda e, hf=hf: e.tensor_tensor(out=xw[:, hf * 512:(hf + 1) * 512], in0=pY, in1=xr[:, hf * 512:(hf + 1) * 512], op=ALU.add),
                                     reads=[R_Y, Rxr], **({"writes": [Rxw]} if hf == 0 else {"pwrites": [Rxw]}))
                            P.dma("pool", "so%d" % (j % 2), lambda e: e.dma_start(out=xmid.ap()[row0:row0 + 128, :], in_=xw), reads=[Rxw], writes=[hres(xmid, row0)])

                        import collections as _co
                        tasks = _co.deque()
                        ntasks = _co.deque()

                        def _step(dq):
                            while dq:
                                try:
                                    next(dq[0])
                                    return True
                                except StopIteration:
                                    dq.popleft()
                            return False

                        def hook():
                            if _step(ntasks):
                                return
                            _step(tasks)

                        def flush_ntasks():
                            while ntasks:
                                for _ in ntasks.popleft():
                                    pass

                        def flush_tasks():
                            flush_ntasks()
                            while tasks:
                                for _ in tasks.popleft():
                                    pass

                        for j in range(4):
                            tasks.append(qprep_gen(0, j))
                        flush_tasks()

                        def norm_gen(c):
                            kind, h, ob, osb, otc, Rotc = c["kind"], c["h"], c["ob"], c["osb"], c["otc"], c["Rotc"]
                            pN = pview(6, F32, [4, 128])
                            for jj in range(4):
                                P.op("pe", lambda e, jj=jj: e.transpose(out=pN[:, jj, 0:65], in_=osb[:, jj * 128:(jj + 1) * 128], identity=ident_f[0:65, 0:65]),
                                     reads=[R_Osb[ob], R_consts], pwrites=[R_T])
                            P.op("dve", lambda e: e.reciprocal(out=rz[:, 0:4], in_=pN[:, :, 64]), reads=[R_T], writes=[R_rz])
                            rzb = rz[:, 0:4].unsqueeze(2).to_broadcast([128, 4, 64])
                            if kind != "D":
                                slot_f = (0 if kind in ("A", "C") else 256) + 64 * h
                                P.op("dve", lambda e: e.tensor_tensor(out=otc[:, :, slot_f:slot_f + 64], in0=pN[:, :, 0:64], in1=rzb, op=ALU.mult),
                                     reads=[R_T, R_rz], pwrites=[Rotc])
                                return
                            cc = c["cc"]
                            P.op("dve", lambda e: e.tensor_tensor(out=od[:, :, cc, :], in0=pN[:, :, 0:64], in1=rzb, op=ALU.mult),
                                 reads=[R_T, R_rz], **({"writes": [R_od]} if cc == 0 else {"pwrites": [R_od]}))
                            if cc == 0:
                                return
                            P.op("dve", lambda e: e.scalar_tensor_tensor(out=od[:, :, 0, :], in0=od[:, :, 1, :], scalar=neglam, in1=od[:, :, 0, :], op0=ALU.mult, op1=ALU.add),
                                 reads=[R_sm, R_od], pwrites=[R_od])
                            P.op("dve", lambda e: e.tensor_tensor(out=dtmp, in0=od[:, :, 0, :], in1=od[:, :, 0, :], op=ALU.mult), reads=[R_od], writes=[R_dtmp])
                            P.op("dve", lambda e: e.tensor_reduce(out=sst2[:, 0:4], in_=dtmp, axis=AX.X, op=ALU.add), reads=[R_dtmp], writes=[R_sst2])
                            yield
                            rstd_small(sst2[:, 0:4], sst2[:, 8:12], sst2[:, 4:8], 4, R_sst2, 64.0)
                            P.op("dve", lambda e: e.tensor_tensor(out=dtmp, in0=od[:, :, 0, :], in1=sst2[:, 8:12].unsqueeze(2).to_broadcast([128, 4, 64]), op=ALU.mult),
                                 reads=[R_od, R_sst2], writes=[R_dtmp])
                            P.op("dve", lambda e: e.tensor_tensor(out=otc[:, :, 256 + 64 * h:320 + 64 * h], in0=dtmp, in1=gsub.unsqueeze(1).to_broadcast([128, 4, 64]), op=ALU.mult),
                                 reads=[R_dtmp, R_sm], pwrites=[Rotc])

                        def emit_qk(c, unit, slot):
                            kind, h, qb = c["kind"], c["h"], c["qb"]
                            kt_ap, q_ap, RK, RQ, Q2c = c["kt_ap"], c["q_ap"], c["RK"], c["RQ"], c["Q2c"]
                            for ui, kt in enumerate(unit):
                                bank = slot * 2 + ui
                                pS = pview(bank, F32, [512])
                                tt = kt - 4 * qb
                                if kind in ("B", "C"):
                                    P.op("pe", lambda e: e.matmul(pS, lhsT=kt_ap(kt), rhs=q_ap(0, 512), start=True, stop=True),
                                         reads=[RK, RQ], writes=[R_S[bank]])
                                elif kind == "A":
                                    c0, c1 = max(0, 128 * (tt - 1)), min(512, 128 * (tt + 2))
                                    u0 = c0 - 128 * tt + 128
                                    P.op("pe", lambda e: e.matmul(pS[:, c0:c1], lhsT=kt_ap(kt), rhs=q_ap(c0, c1), start=True, stop=False, skip_group_check=True),
                                         reads=[RK, RQ], writes=[R_S[bank]])
                                    P.op("pe", lambda e: e.matmul(pS[:, c0:c1], lhsT=ident_bf, rhs=bias_tab[:, h, u0:u0 + (c1 - c0)], start=False, stop=True, skip_group_check=True),
                                         reads=[R_tab, R_consts], pwrites=[R_S[bank]])
                                else:
                                    cc = c["cc"]
                                    ks = slice(kt * 128, (kt + 1) * 128)
                                    if tt < 0 or tt > 3:
                                        var = 0 if tt < 0 else 1
                                        P.op("pe", lambda e: e.matmul(pS, lhsT=KT2[0:68, h, ks], rhs=Q2c[0:68, h, cc, var, 0:512], start=True, stop=True),
                                             reads=[RK, RQ], writes=[R_S[bank]])
                                    else:
                                        firstm = True
                                        for jj in range(4):
                                            cs = slice(128 * jj, 128 * jj + 128)
                                            wkw = {"writes": [R_S[bank]]} if firstm else {"pwrites": [R_S[bank]]}
                                            if jj == tt:
                                                P.op("pe", lambda e: e.matmul(pS[:, cs], lhsT=KT2[0:64, h, ks], rhs=Q2c[0:64, h, cc, 0, cs], start=firstm, stop=False, skip_group_check=True),
                                                     reads=[RK, RQ], **wkw)
                                                P.op("pe", lambda e: e.matmul(pS[:, cs], lhsT=ident_bf, rhs=bias_tab[:, h, :], start=False, stop=True, skip_group_check=True),
                                                     reads=[R_tab, R_consts], pwrites=[R_S[bank]])
                                            else:
                                                var = 1 if jj < tt else 0
                                                P.op("pe", lambda e: e.matmul(pS[:, cs], lhsT=KT2[0:68, h, ks], rhs=Q2c[0:68, h, cc, var, cs], start=firstm, stop=(jj == 3), skip_group_check=True),
                                                     reads=[RK, RQ], **wkw)
                                            firstm = False

                        def emit_exp(c, unit, slot):
                            kind, qb = c["kind"], c["qb"]
                            pb = pt_i[0] % NPB
                            pt_i[0] += 1
                            if kind == "A":
                                for ui, kt in enumerate(unit):
                                    bank = slot * 2 + ui
                                    tt = kt - 4 * qb
                                    c0, c1 = max(0, 128 * (tt - 1)), min(512, 128 * (tt + 2))
                                    pS = pview(bank, F32, [512])
                                    P.op("act", lambda e: e.activation(out=PT[pb][:, ui, c0:c1], in_=pS[:, c0:c1], func=ACTF.Exp),
                                         reads=[R_S[bank]], **({"writes": [R_PT[pb]]} if ui == 0 else {"pwrites": [R_PT[pb]]}))
                            else:
                                n = len(unit)
                                pS2 = pview(slot * 2, F32, [n, 512])
                                P.op("act", lambda e: e.activation(out=PT[pb][:, 0:n, :], in_=pS2, func=ACTF.Exp),
                                     reads=[R_S[slot * 2 + a] for a in range(n)], writes=[R_PT[pb]])
                            return pb

                        def emit_pv(c, unit, pb, first_unit, last_unit):
                            kind, qb, ob, v_ap = c["kind"], c["qb"], c["ob"], c["v_ap"]
                            pO = c["pO"]
                            for ui, kt in enumerate(unit):
                                st = first_unit and ui == 0
                                sp = last_unit and ui == len(unit) - 1
                                if kind == "A":
                                    tt = kt - 4 * qb
                                    c0, c1 = max(0, 128 * (tt - 1)), min(512, 128 * (tt + 2))
                                else:
                                    c0, c1 = 0, 512
                                P.op("pe", lambda e: e.matmul(pO[:, c0:c1], lhsT=v_ap(kt), rhs=PT[pb][:, ui, c0:c1], start=st, stop=sp, skip_group_check=True),
                                     reads=[R_V, R_PT[pb]], **({"writes": [R_O[ob]]} if st else {"pwrites": [R_O[ob]]}))
                            if last_unit:
                                osb, h = c["osb"], c["h"]
                                while len(ntasks) >= 2:
                                    for _ in ntasks.popleft():
                                        pass
                                P.op("dve", lambda e: e.tensor_copy(out=osb, in_=pO), reads=[R_O[ob]], writes=[R_Osb[ob]])
                                if kind == "A":
                                    P.op("dve", lambda e: e.tensor_scalar(out=osb[64:65, :], in0=osb[64:65, :], scalar1=esink[64:65, h:h + 1], scalar2=None, op0=ALU.add),
                                         reads=[R_sm, R_Osb[ob]], pwrites=[R_Osb[ob]])
                                ntasks.append(norm_gen(c))

                        def make_ctx(mp, qb):
                            qi = qb % 2
                            Q1c, Q2c, RQ1, RQ2 = Q1[qi], Q2[qi], R_Q1[qi], R_Q2[qi]
                            kind, h = mp[0], mp[1]
                            ob = o_i[0] % 2
                            o_i[0] += 1
                            c = {"kind": kind, "h": h, "qb": qb, "ob": ob, "pO": pview(4 + ob, F32, [512], 0, 65), "osb": Osb[ob],
                                 "otc": otm[qi], "Rotc": R_otm[qi], "Q2c": Q2c, "cc": (mp[2] if kind == "D" else 0)}
                            if kind in ("A", "C"):
                                g = h // 2
                                c.update(kt_ap=lambda kt: KT1[:, kt * 128:(kt + 1) * 128], q_ap=lambda c0, c1: Q1c[:, h, c0:c1],
                                         v_ap=lambda kt: Vst[:, kt, g, :], RK=R_KT1, RQ=RQ1)
                            elif kind == "B":
                                c.update(kt_ap=lambda kt: KT2[:, h, kt * 128:(kt + 1) * 128], q_ap=lambda c0, c1: Q2c[:, h, c0:c1],
                                         v_ap=lambda kt: Vst[:, kt, 2 + h, :], RK=R_KT2, RQ=RQ2)
                            else:
                                c.update(kt_ap=None, q_ap=None, v_ap=lambda kt: Vst[:, kt, 2 + h, :], RK=R_KT2, RQ=RQ2)
                            if kind == "A":
                                kts = [kt for kt in range(4 * qb - 1, 4 * qb + 5) if 0 <= kt < NT]
                            else:
                                kts = list(range(NT))
                            c["units"] = [kts[a:a + 2] for a in range(0, len(kts), 2)]
                            return c

                        for qb in range(NQB):
                            if qb + 1 < NQB:
                                for j in range(4):
                                    tasks.append(qprep_gen(qb + 1, j))
                            if grp == 0:
                                maps = [("A", h) for h in range(4)] + [("B", h) for h in range(4)]
                            else:
                                maps = [("C", h) for h in range(4)] + [("D", h, cc) for h in range(4) for cc in range(2)]
                            pipe = _co.deque()
                            gi = 0
                            for mp in maps:
                                c = make_ctx(mp, qb)
                                nu = len(c["units"])
                                for ui_, unit in enumerate(c["units"]):
                                    slot = gi % 2
                                    emit_qk(c, unit, slot)
                                    pb = emit_exp(c, unit, slot)
                                    pipe.append((c, unit, pb, ui_ == 0, ui_ == nu - 1))
                                    if len(pipe) > 2:
                                        emit_pv(*pipe.popleft())
                                    if gi % 2 == 1 or NT <= 16:
                                        hook()
                                    gi += 1
                            while pipe:
                                emit_pv(*pipe.popleft())
                            flush_tasks()
                            for j in range(4):
                                tasks.append(outproj_gen(qb, j))
                            if qb + 1 < NQB:
                                pass
                            else:
                                flush_tasks()

                        grp_tail = [R_KT1, R_KT2, R_V, R_tab, R_sm, R_wq, R_wo, fr.RxnT, fr.Rxn, fr.Rjunk, fr.Rst, R_stg, R_stgb, R_tmp, R_sst, R_cqT,
                                    R_rz, R_od, R_dtmp, R_oT, R_sst2] + R_Q1 + R_Q2 + R_otm + fr.Rx + wl.R + R_xres + R_PT + R_Osb + R_xo
                        prev_tail[0] = grp_tail
                        if KSTOP == 2 + 2 * grp:
                            raise _Stop()

                    barrier(prev_tail[0])
                    af = Alloc(PERS_END)
                    gT = af.take(BF16, [NFC, 512])
                    _go = af.last // 4
                    wl = WLoader(af, cast_engs=("dve", "act", "pool"), slots=[arena[0:128, _go + 512 * i_: _go + 512 * (i_ + 1)] for i_ in range(8)])
                    wu = af.take(BF16, [8, 2 * DFF])
                    wd = af.take(BF16, [NFC, D])
                    R_wu, R_wd = Res(), Res()
                    R_fs = Res()
                    gcol = af.take(F32, [8])
                    bup = af.take(F32, [44])
                    cw = af.take(F32, [3, 44])
                    cb = af.take(F32, [44])
                    b1 = af.take(F32, [44])
                    gfin = af.take(F32, [D])
                    P.dma("sync", "c", lambda e: e.dma_start(out=gcol, in_=g_ffn_p.ap()[l]), pwrites=[R_fs])
                    P.dma("sync", "c", lambda e: e.dma_start(out=bup, in_=b_up_p.ap()[l]), pwrites=[R_fs])
                    P.dma("sync", "c", lambda e: e.dma_start(out=cw, in_=conv_w_p.ap()[l].rearrange("p (a b) -> p a b", a=3)), pwrites=[R_fs])
                    P.dma("sync", "c", lambda e: e.dma_start(out=cb, in_=conv_b_p.ap()[l]), pwrites=[R_fs])
                    P.dma("sync", "c", lambda e: e.dma_start(out=gfin, in_=g_final.ap().partition_broadcast(128)), pwrites=[R_fs])
                    P.op("dve", lambda e: e.tensor_tensor(out=b1, in0=bup, in1=cw[:, 1, :], op=ALU.mult), reads=[R_fs], pwrites=[R_fs])
                    P.op("dve", lambda e: e.tensor_tensor(out=b1, in0=b1, in1=cb, op=ALU.add), reads=[R_fs], pwrites=[R_fs])
                    for k in range(8):
                        for c0 in range(0, 2 * DFF, 512):
                            n = min(512, 2 * DFF - c0)
                            wl.load(wu[:, k, c0:c0 + n], w_up.ap()[l, k * 128:(k + 1) * 128, c0:c0 + n], n, scale=gcol[:, k:k + 1], Rdst=R_wu, Rscale=R_fs)
                    for c in range(NFC):
                        for hf in range(2):
                            wl.load(wd[:, c, hf * 512:(hf + 1) * 512], w_down.ap()[l, c * 128:(c + 1) * 128, hf * 512:(hf + 1) * 512], 512, Rdst=R_wd)
                    fr = Front(af, nx=1)
                    x2T_b = [af.take(BF16, [8, 512]) for _ in range(2)]
                    R_x2T_b = [Res(), Res()]
                    hb = [af.take(F32, [512]) for _ in range(2)]
                    R_hb = [Res() for _ in range(2)]
                    t1 = [af.take(F32, [512]) for _ in range(2)]
                    R_t1 = [Res() for _ in range(2)]
                    sa = af.take(F32, [512])
                    sa_bf = arena_bf[0:128, af.last // 2: af.last // 2 + 1024]
                    R_sa = Res()
                    R_gT = Res()
                    P.op("dve", lambda e: e.memset(gT[:, 0, 0:2], 0.0), writes=wl.R + [R_gT])
                    xr2 = [af.take(F32, [D]) for _ in range(2)]
                    R_xr2 = [Res() for _ in range(2)]

                    if os.environ.get('KDEBUG'):
                        print('FFN alloc end', S, af.off)
                    ob_ = 0
                    blocks = []
                    blk0 = 0
                    while blk0 < S:
                        nout = min(510, S - blk0)
                        blocks.append((blk0, nout))
                        blk0 += nout

                    def x2T_gen(bi):
                        blk0, nout = blocks[bi]
                        x2T, R_x2T = x2T_b[bi % 2], R_x2T_b[bi % 2]
                        lo = blk0 - 1
                        ncol = nout + 2
                        ci = 0
                        while ci < ncol:
                            tok = lo + ci
                            if tok < 0:
                                P.op("dve", lambda e: e.memset(x2T[:, :, ci:ci + 1], 0.0), pwrites=[R_x2T])
                                ci += 1
                                continue
                            nr = min(128, ncol - ci, S - tok)
                            if nr <= 0:
                                P.op("dve", lambda e: e.memset(x2T[:, :, ci:ncol], 0.0), pwrites=[R_x2T])
                                break
                            xt, Rx = fr.load(xmid, row_base + tok, nr)
                            yield
                            fr.norm_a(xt, Rx)
                            yield
                            fr.norm_b()
                            P.op("dve", lambda e: e.tensor_copy(out=x2T[:, :, ci:ci + nr], in_=fr.xnT[:, :, 0:nr]), reads=[fr.RxnT], pwrites=[R_x2T])
                            yield
                            ci += nr

                    import collections as _co2
                    ftasks = _co2.deque()

                    def fstep():
                        while ftasks:
                            try:
                                next(ftasks[0])
                                return
                            except StopIteration:
                                ftasks.popleft()

                    def fflush():
                        while ftasks:
                            for _ in ftasks.popleft():
                                pass

                    ftasks.append(x2T_gen(0))
                    fflush()
                    for bi, (blk0, nout) in enumerate(blocks):
                        x2T, R_x2T = x2T_b[bi % 2], R_x2T_b[bi % 2]
                        if bi + 1 < len(blocks):
                            ftasks.append(x2T_gen(bi + 1))
                        lo = blk0 - 1
                        ncol = nout + 2
                        pad_lo = (lo < 0)
                        pad_hi = (lo + ncol > S)
                        for i in range(NFC):
                            for half in range(2):
                                fc = i + NFC * half
                                bank = (2 * i + half) % 4
                                pU = pview(bank, F32, [512])
                                for k in range(8):
                                    P.op("pe", lambda e, k=k, fc=fc, pU=pU, ncol=ncol: e.matmul(pU[:, 0:ncol], lhsT=wu[:, k, fc * 128:(fc + 1) * 128], rhs=x2T[:, k, 0:ncol],
                                                                                              start=(k == 0), stop=(k == 7)),
                                         reads=[R_wu, R_x2T], pwrites=[R_S[bank]])
                                hbt, Rh = hb[half], R_hb[half]
                                t1t, Rt = t1[half], R_t1[half]
                                P.op("act", lambda e, hbt=hbt, pU=pU, fc=fc, ncol=ncol: e.activation(out=hbt[:, 0:ncol], in_=pU[:, 0:ncol], func=ACTF.Identity, bias=bup[:, fc:fc + 1]),
                                     reads=[R_S[bank], R_fs], writes=[Rh])
                                if pad_lo:
                                    P.op("pool", lambda e, hbt=hbt: e.memset(hbt[:, 0:1], 0.0), pwrites=[Rh])
                                if pad_hi:
                                    P.op("pool", lambda e, hbt=hbt, ncol=ncol: e.memset(hbt[:, ncol - 1:ncol], 0.0), pwrites=[Rh])
                                P.op("act", lambda e, t1t=t1t, pU=pU, fc=fc, nout=nout: e.activation(out=t1t[:, 0:nout], in_=pU[:, 1:1 + nout], func=ACTF.Identity,
                                                                                                     bias=b1[:, fc:fc + 1], scale=cw[:, 1, fc:fc + 1]),
                                     reads=[R_S[bank], R_fs], writes=[Rt])
                                P.op("dve", lambda e, t1t=t1t, hbt=hbt, fc=fc, nout=nout: e.scalar_tensor_tensor(out=t1t[:, 0:nout], in0=hbt[:, 0:nout], scalar=cw[:, 0, fc:fc + 1],
                                                                                                                 in1=t1t[:, 0:nout], op0=ALU.mult, op1=ALU.add),
                                     reads=[Rh, R_fs, Rt], pwrites=[Rt])
                                P.op("dve", lambda e, t1t=t1t, hbt=hbt, fc=fc, nout=nout: e.scalar_tensor_tensor(out=t1t[:, 0:nout], in0=hbt[:, 2:2 + nout], scalar=cw[:, 2, fc:fc + 1],
                                                                                                                 in1=t1t[:, 0:nout], op0=ALU.mult, op1=ALU.add),
                                     reads=[Rh, R_fs, Rt], pwrites=[Rt])
                            P.op("act", lambda e, nout=nout: e.activation(out=sa[:, 0:nout], in_=t1[0][:, 0:nout], func=ACTF.Silu), reads=[R_t1[0]], writes=[R_sa])
                            P.op("dve", lambda e, i=i, nout=nout: e.tensor_tensor(out=gT[:, i, 0:nout], in0=sa[:, 0:nout], in1=t1[1][:, 0:nout], op=ALU.mult),
                                 reads=[R_sa, R_t1[1]], pwrites=[R_gT])
                        for s0 in range(0, nout, 128):
                            ns = min(128, nout - s0)
                            row0 = row_base + blk0 + s0
                            oi = ob_ % 2
                            xr = xr2[oi]
                            Rxr = R_xr2[oi]
                            xw = xr
                            Rxw = Rxr
                            ob_ += 1
                            rr = [hres(xmid, r) for r in range((row0 // 128) * 128, row0 + ns, 128)]
                            P.dma("sync", "xr%d" % oi, lambda e, xr=xr, row0=row0, ns=ns: e.dma_start(out=xr[0:ns, :], in_=xmid.ap()[row0:row0 + ns, :]), reads=rr, writes=[Rxr])
                            for hf in range(2):
                                fstep()
                                fstep()
                                if hf == 0:
                                    pY, RY = pview(7, F32, [512]), R_Y
                                else:
                                    pY, RY = pview(4 + (ob_ % 2), F32, [512]), R_O[ob_ % 2]
                                for c in range(NFC):
                                    P.op("pe", lambda e, c=c, hf=hf, pY=pY, s0=s0, ns=ns: e.matmul(pY[0:ns, :], lhsT=gT[:, c, s0:s0 + ns], rhs=wd[:, c, hf * 512:(hf + 1) * 512],
                                                                                                 start=(c == 0), stop=(c == NFC - 1)),
                                         reads=[R_gT, R_wd], pwrites=[RY])
                                P.op("dve", lambda e, hf=hf, pY=pY, xw=xw, xr=xr, ns=ns: e.tensor_tensor(out=xw[0:ns, hf * 512:(hf + 1) * 512], in0=pY[0:ns, :], in1=xr[0:ns, hf * 512:(hf + 1) * 512], op=ALU.add),
                                     reads=[RY, Rxr], pwrites=[Rxw])
                            if dst_t is not None:
                                wr = [hres(dst_t, r) for r in range((row0 // 128) * 128, row0 + ns, 128)]
                                P.dma("pool", "so%d" % oi, lambda e, xw=xw, row0=row0, ns=ns: e.dma_start(out=dst_t.ap()[row0:row0 + ns, :], in_=xw[0:ns, :]), reads=[Rxw], pwrites=wr)
                            else:
                                fr.final_norm_store(xw, Rxw, gfin, R_fs, row0, ns, "so%d" % oi, sa_bf, R_sa)
                        fflush()
                    ffn_tail = [R_wu, R_wd, R_fs, fr.RxnT, fr.Rxn, fr.Rjunk, fr.Rst, R_sa, R_gT] + R_x2T_b + fr.Rx + wl.R + R_hb + R_t1 + R_xr2
                    prev_tail[0] = ffn_tail

        try:
            build_body()
        except _Stop:
            pass
        P.emit(final_waits=stores)
    return nc


_CACHE = {}


def _get_nc(S_list):
    key = tuple(S_list)
    if key not in _CACHE:
        _CACHE[key] = build_program(list(S_list))
    return _CACHE[key]


def _pcol(v, k):
    Lc = v.shape[0]
    return np.ascontiguousarray(v.reshape(Lc, k, 128).transpose(0, 2, 1)).astype(np.float32)


def kernel(x_prompt, x_sample, g_attn, w_in, a_sink, b_q_norm, b_w_q_up, b_kv_norm, b_w_kv_up,
           c_q_norm, c_k_norm, d_lambda_q1, d_lambda_k1, d_lambda_q2, d_lambda_k2, d_sub_norm, w_out,
           g_ffn, w_up, b_up, conv_w, conv_b, w_down, g_final):
    f = lambda a: np.ascontiguousarray(np.asarray(a, dtype=np.float32))
    x_prompt, x_sample = f(x_prompt), f(x_sample)
    nb = x_prompt.shape[0]
    S_list = [x_prompt.shape[1], x_sample.shape[1]]
    nc = _get_nc(S_list)
    consts = make_consts(max(S_list))
    conv_w_f = f(conv_w)
    cwp = np.stack([_pcol(conv_w_f[:, j, :], 44) for j in range(3)], axis=2)
    shared = {
        "w_in": f(w_in), "w_out": f(w_out), "w_up": f(w_up), "w_down": f(w_down),
        "b_w_q_up": f(b_w_q_up), "b_w_kv_up": f(b_w_kv_up),
        "g_attn_p": _pcol(f(g_attn), 8), "g_ffn_p": _pcol(f(g_ffn), 8),
        "bqn_p": _pcol(f(b_q_norm), 2), "bkvn_p": _pcol(f(b_kv_norm), 1),
        "b_up_p": _pcol(f(b_up), 44), "conv_w_p": np.ascontiguousarray(cwp.reshape(cwp.shape[0], 128, 132)),
        "conv_b_p": _pcol(f(conv_b), 44),
        "a_sink": f(a_sink), "c_q_norm": f(c_q_norm), "c_k_norm": f(c_k_norm), "d_sub_norm": f(d_sub_norm),
        "lamv": np.ascontiguousarray(np.concatenate([f(d_lambda_q1), f(d_lambda_k1), f(d_lambda_q2), f(d_lambda_k2)], axis=1)),
        "g_final": f(g_final).reshape(1, D),
    }
    shared.update(consts)
    in_maps = []
    for b in range(nb):
        m = dict(shared)
        m["x"] = np.ascontiguousarray(np.concatenate([x_prompt[b], x_sample[b]], axis=0))
        in_maps.append(m)
    res = run_bass_kernel_spmd(nc, in_maps, core_ids=list(range(nb)))
    ys = [np.asarray(r["y"], dtype=np.float32) for r in res.results]
    yp = np.stack([y[:S_list[0]] for y in ys], axis=0)
    ysmp = np.stack([y[S_list[0]:] for y in ys], axis=0)
    return (yp, ysmp)
```

```python
import contextlib
import math
import numpy as np
import ml_dtypes
import concourse.bass as bass
import concourse.mybir as mybir
from concourse.bass_utils import run_bass_kernel_spmd

F32 = mybir.dt.float32
BF16 = mybir.dt.bfloat16
ALU = mybir.AluOpType
ACTF = mybir.ActivationFunctionType
AX = mybir.AxisListType

D = 1024
L = 2
DFF = 2816
NFC = 22
EPS = 1e-6
NEG = -30000.0
SLOPES = [2.0 ** (-8.0 * (i + 1.0) / 8.0) for i in range(8)]


class Op:
    __slots__ = ("eng", "fn", "deps", "sig", "val", "dkey", "is_dma")

    def __init__(self, eng, fn, deps, dkey=None):
        self.eng = eng
        self.fn = fn
        self.deps = deps
        self.sig = False
        self.val = 0
        self.dkey = dkey
        self.is_dma = dkey is not None


class _Rec:
    def __init__(self):
        self.call = None

    def __getattr__(self, name):
        def f(*args, **kwargs):
            self.call = (name, args, kwargs)
            return None
        return f


def _bind(fn):
    r = _Rec()
    fn(r)
    name, args, kwargs = r.call
    return lambda eng: getattr(eng, name)(*args, **kwargs)


class Res:
    __slots__ = ("w", "r", "pw", "pr", "open", "excl")

    BAR = [None]

    def __init__(self):
        self.w = [Res.BAR[0]] if Res.BAR[0] is not None else []
        self.r = []
        self.pw = []
        self.pr = []
        self.open = False
        self.excl = False


class _Stop(Exception):
    pass


import os
KSTOP = int(os.environ.get('KSTOP', '0'))


def ckpt(n):
    if KSTOP == n:
        raise _Stop()


class Prog:
    ENGS = ("sync", "act", "dve", "pool", "pe")

    def __init__(self, nc):
        self.nc = nc
        self.q = {e: [] for e in self.ENGS}
        self.dma_keys = {}
        self.dma_rr = {}
        self.dma_last = {}

    def _deps(self, eng, reads, writes, pwrites, extra):
        deps = list(extra)
        for R in reads:
            deps += R.w
            if R.excl:
                deps += [r for r in R.r if r.eng != eng]
        for R in writes:
            R.pw, R.pr = R.w, R.r
            R.w, R.r = [], []
            R.open = False
            deps += R.pw + R.pr
        for R in pwrites:
            if R.r or not R.open:
                R.pw, R.pr = R.w, R.r
                R.w, R.r = [], []
                R.open = True
            deps += R.pw + R.pr
        return [d for d in deps if d is not None]

    def _post(self, o, reads, writes, pwrites):
        for R in reads:
            R.r.append(o)
            R.open = False
        for R in writes:
            R.w.append(o)
        for R in pwrites:
            R.w.append(o)

    def op(self, eng, fn, reads=(), writes=(), pwrites=(), extra=()):
        deps = self._deps(eng, reads, writes, pwrites, extra)
        if eng == "pe":
            deps = [d for d in deps if d.is_dma or d.eng != "pe"]
        o = Op(eng, _bind(fn), deps)
        self.q[eng].append(o)
        self._post(o, reads, writes, pwrites)
        return o

    def dma(self, eng, key, fn, reads=(), writes=(), pwrites=(), extra=()):
        if key == "c":
            i = self.dma_rr.get("c", 0)
            self.dma_rr["c"] = i + 1
            key = "c%d" % (i % 8)
        deps = self._deps(eng, reads, writes, pwrites, extra)
        if key in self.dma_last:
            deps.append(self.dma_last[key])
        o = Op(eng, _bind(fn), deps, dkey=key)
        self.dma_keys[key] = self.dma_keys.get(key, 0) + 16
        o.val = self.dma_keys[key]
        self.q[eng].append(o)
        self.dma_last[key] = o
        self._post(o, reads, writes, pwrites)
        return o

    def emit(self, final_waits=()):
        nc = self.nc
        for e in self.ENGS:
            for o in self.q[e]:
                for d in o.deps:
                    if not d.is_dma:
                        d.sig = True
        for e in self.ENGS:
            c = 0
            for o in self.q[e]:
                if (not o.is_dma) and o.sig:
                    c += 1
                    o.val = c
        with contextlib.ExitStack() as es:
            esem = {e: es.enter_context(nc.semaphore("tl_" + e)) for e in self.ENGS}
            dsem = {k: es.enter_context(nc.semaphore("dq_%d" % i)) for i, k in enumerate(self.dma_keys)}
            block = es.enter_context(nc.Block())
            final_waits = list(final_waits)

            def run(ename):
                def body(eng):
                    waited = {}

                    def waits(deps):
                        need = {}
                        for d in deps:
                            key = ("d", d.dkey) if d.is_dma else ("e", d.eng)
                            if need.get(key, 0) < d.val:
                                need[key] = d.val
                        for key, v in need.items():
                            if waited.get(key, 0) < v:
                                sem = dsem[key[1]] if key[0] == "d" else esem[key[1]]
                                eng.wait_ge(sem, v)
                                waited[key] = v

                    for o in self.q[ename]:
                        waits(o.deps)
                        ins = o.fn(eng)
                        if o.is_dma:
                            ins.then_inc(dsem[o.dkey], 16)
                        elif o.sig:
                            ins.then_inc(esem[ename], 1)
                    if ename == "sync":
                        waits(final_waits)
                        if KSTOP != 0:
                            waits(list(self.dma_last.values()))
                return body

            block.sync(run("sync"))
            block.scalar(run("act"))
            block.vector(run("dve"))
            block.gpsimd(run("pool"))
            block.tensor(run("pe"))


def make_consts(smax):
    bf = ml_dtypes.bfloat16
    nt = smax // 128
    c = {}
    c["ident_bf"] = np.eye(128, dtype=np.float32).astype(bf)
    c["ident_f"] = np.eye(128, dtype=np.float32)
    inv = (10000.0 ** (-np.arange(16, dtype=np.float32) * 2.0 / 32.0)).astype(np.float32)
    pos = np.arange(smax, dtype=np.float32)

    def tab(p):
        ang = (p[:, None] * inv[None, :]).astype(np.float32)
        cs, sn = np.cos(ang).astype(np.float32), np.sin(ang).astype(np.float32)
        t = np.concatenate([cs, -sn, sn], axis=1)
        return np.ascontiguousarray(t.reshape(nt, 128, 48).transpose(1, 0, 2))

    c["rope1d"] = tab(pos)
    c["roperow"] = tab(np.floor(pos / 64.0).astype(np.float32))
    c["ropecol"] = tab(np.mod(pos, 64.0).astype(np.float32))
    ki = np.arange(128)[:, None]
    u = np.arange(384)[None, :]
    dist = np.abs((u - 128) - ki).astype(np.float32)
    sa = np.stack([np.where(dist <= 128, -SLOPES[h] * dist, NEG) for h in range(4)]).astype(np.float32)
    c["stripA"] = np.ascontiguousarray(sa.transpose(1, 0, 2)).astype(bf)
    qi = np.arange(128)[None, :]
    dd = np.abs(qi - ki).astype(np.float32)
    td = np.stack([-SLOPES[4 + h] * dd for h in range(4)]).astype(np.float32)
    c["toepD"] = np.ascontiguousarray(td.transpose(1, 0, 2)).astype(bf)
    kp = np.arange(smax)
    c["kaugD"] = np.stack([np.ones(smax), np.ones(smax), 128.0 * (kp // 128), (kp % 128)]).astype(np.float32).astype(bf)
    qh = (kp // 256).astype(np.float32)
    ql = (kp % 256).astype(np.float32)
    qa = np.zeros((4, 4, 2, 2, smax), np.float32)
    for h in range(4):
        m = SLOPES[4 + h]
        bef = np.stack([-m * 256.0 * qh, -m * ql, m * np.ones(smax), m * np.ones(smax)])
        for cc in range(2):
            qa[:, h, cc, 0, :] = bef
            qa[:, h, cc, 1, :] = -bef
    c["qaugD"] = qa.astype(bf)
    return c


CONST_SHAPES = None


def build_program(S_list, nlayers=L):
    nc = bass.Bass("TRN2", target_bir_lowering=False)
    TT = sum(S_list)
    SMAX = max(S_list)
    NTMAX = SMAX // 128
    P = Prog(nc)
    Res.BAR[0] = None

    def din(name, shape, dt=F32):
        return nc.dram_tensor(name, list(shape), dt, kind="ExternalInput")

    x_in = din("x", [TT, D])
    y_out = nc.dram_tensor("y", [TT, D], F32, kind="ExternalOutput")
    xmid = nc.dram_tensor("xmid", [TT, D], F32)
    x1 = nc.dram_tensor("x1", [TT, D], F32)
    w_in = din("w_in", [L, D, 2208])
    w_out = din("w_out", [L, D, D])
    w_up = din("w_up", [L, D, 2 * DFF])
    w_down = din("w_down", [L, DFF, D])
    w_qup = din("b_w_q_up", [L, 256, 384])
    w_kvup = din("b_w_kv_up", [L, 128, 512])
    g_attn_p = din("g_attn_p", [L, 128, 8])
    g_ffn_p = din("g_ffn_p", [L, 128, 8])
    bqn_p = din("bqn_p", [L, 128, 2])
    bkvn_p = din("bkvn_p", [L, 128, 1])
    b_up_p = din("b_up_p", [L, 128, 44])
    conv_w_p = din("conv_w_p", [L, 128, 3 * 44])
    conv_b_p = din("conv_b_p", [L, 128, 44])
    a_sink = din("a_sink", [L, 4])
    cqn = din("c_q_norm", [L, 64])
    ckn = din("c_k_norm", [L, 64])
    dsn = din("d_sub_norm", [L, 64])
    lamv = din("lamv", [L, 128])
    g_final = din("g_final", [1, D])
    c_ident_bf = din("ident_bf", [128, 128], BF16)
    c_ident_f = din("ident_f", [128, 128])
    c_rope1d = din("rope1d", [128, NTMAX, 48])
    c_roperow = din("roperow", [128, NTMAX, 48])
    c_ropecol = din("ropecol", [128, NTMAX, 48])
    c_stripA = din("stripA", [128, 4, 384], BF16)
    c_toepD = din("toepD", [128, 4, 128], BF16)
    c_kaugD = din("kaugD", [4, SMAX], BF16)
    c_qaugD = din("qaugD", [4, 4, 2, 2, SMAX], BF16)

    es = contextlib.ExitStack()
    with es:
        ARENA_F32 = 52800
        arena = es.enter_context(nc.sbuf_tensor("arena", [128, ARENA_F32], F32))
        arena_bf = arena.bitcast(BF16)
        psum = es.enter_context(nc.psum_tensor("psum", [128, 4096], F32))
        psum_bf = psum.bitcast(BF16)

        class Alloc:
            def __init__(self, base=0):
                self.off = base

            def take(self, dt, shape, p0=0, p1=128):
                n = int(np.prod(shape))
                esz = 4 if dt == F32 else 2
                self.off = (self.off + 31) // 32 * 32
                o = self.off // esz
                self.last = self.off
                self.off += n * esz
                assert self.off <= ARENA_F32 * 4, ("SBUF arena overflow", self.off)
                h = arena if dt == F32 else arena_bf
                ap = h[p0:p1, o:o + n]
                return view(ap, shape)

        def view(ap, shape):
            if len(shape) == 1:
                return ap
            names = "abcde"[:len(shape)]
            kw = {names[i]: int(shape[i]) for i in range(len(shape) - 1)}
            return ap.rearrange("p (%s) -> p %s" % (" ".join(names), " ".join(names)), **kw)

        def pview(bank, dt, shape, p0=0, p1=128, nbanks=1):
            n = int(np.prod(shape))
            if dt == F32:
                ap = psum[p0:p1, bank * 512: bank * 512 + n]
            else:
                ap = psum_bf[p0:p1, bank * 1024: bank * 1024 + n]
            return view(ap, shape)

        A0 = Alloc(0)
        ident_bf = A0.take(BF16, [128])
        ident_f = A0.take(F32, [128])
        epsc = A0.take(F32, [1])
        vecs = A0.take(F32, [16])
        R_consts = Res()
        P.dma("sync", "c", lambda e: e.dma_start(out=ident_bf, in_=c_ident_bf.ap()), pwrites=[R_consts])
        P.dma("sync", "c", lambda e: e.dma_start(out=ident_f, in_=c_ident_f.ap()), pwrites=[R_consts])
        P.op("pool", lambda e: e.memset(epsc, EPS), pwrites=[R_consts])
        PERS_END = A0.off

        R_S = [Res() for _ in range(4)]
        R_O = [Res() for _ in range(2)]
        R_T = Res()
        R_Y = Res()
        for _r in R_S + R_O + [R_T, R_Y]:
            _r.excl = True

        prev_tail = [[]]

        def barrier(tail):
            o = P.op("pool", lambda e: e.memset(vecs[:, 0:1], 0.0), writes=list(tail))
            Res.BAR[0] = o

        stores = []
        R_hbm = {}

        def hres(t, row0):
            return R_hbm.setdefault((t.name, row0), Res())

        class WLoader:
            def __init__(self, al, dma_eng="sync", cast_engs=("dve",), key="w", slots=None, nslots=2):
                self.st = slots if slots is not None else [al.take(F32, [512]) for _ in range(nslots)]
                self.R = [Res() for _ in self.st]
                self.i = 0
                self.dma_eng, self.cast_engs, self.key = dma_eng, cast_engs, key

            def load(self, dst, src, ncols, scale=None, Rdst=None, Rscale=None):
                s = self.i % len(self.st)
                self.i += 1
                st = self.st[s][:, 0:ncols]
                if len(dst.shape) == 3:
                    st = st.rearrange("p (a b) -> p a b", a=dst.shape[1])
                    assert len(src.shape) == 3
                P.dma(self.dma_eng, "%s%d" % (self.key, s), lambda e: e.dma_start(out=st, in_=src), writes=[self.R[s]])
                eng = self.cast_engs[self.i % len(self.cast_engs)]
                rd = [self.R[s]] + ([Rscale] if Rscale is not None else [])
                if scale is None and eng == "act":
                    P.op(eng, lambda e: e.copy(out=dst, in_=st), reads=rd, pwrites=[Rdst])
                elif scale is None:
                    P.op(eng, lambda e: e.tensor_copy(out=dst, in_=st), reads=rd, pwrites=[Rdst])
                elif eng == "pool":
                    P.op(eng, lambda e: e.tensor_scalar(out=dst, in0=st, scalar1=scale, scalar2=0.0, op0=ALU.mult, op1=ALU.add),
                         reads=rd, pwrites=[Rdst])
                elif eng == "act":
                    P.op(eng, lambda e: e.activation(out=dst, in_=st, func=ACTF.Identity, scale=scale),
                         reads=rd, pwrites=[Rdst])
                else:
                    P.op(eng, lambda e: e.tensor_scalar(out=dst, in0=st, scalar1=scale, scalar2=None, op0=ALU.mult),
                         reads=rd, pwrites=[Rdst])

        class Front:
            def __init__(self, al, nx=2):
                self.xt = [al.take(F32, [D]) for _ in range(nx)]
                self.Rx = [Res() for _ in range(nx)]
                self.xn = al.take(BF16, [D])
                self.Rxn = Res()
                self.junk = self.xn
                self.Rjunk = self.Rxn
                self.xnT = al.take(BF16, [8, 128])
                self.RxnT = Res()
                self.st = al.take(F32, [4])
                self.Rst = Res()
                self.i = 0

            def load(self, src_t, row0, nrows=128):
                s = self.i % len(self.xt)
                self.i += 1
                xt = self.xt[s]
                R = self.Rx[s]
                rr = [hres(src_t, r) for r in range((row0 // 128) * 128, row0 + nrows, 128)] if src_t is not x_in else []
                if nrows < 128:
                    P.op("pool", lambda e: e.memset(xt, 0.0), writes=[R])
                    P.dma("sync", "xt%d" % s, lambda e: e.dma_start(out=xt[0:nrows, :], in_=src_t.ap()[row0:row0 + nrows, :]),
                          reads=rr, pwrites=[R])
                else:
                    P.dma("sync", "xt%d" % s, lambda e: e.dma_start(out=xt, in_=src_t.ap()[row0:row0 + nrows, :]),
                          reads=rr, writes=[R])
                return xt, R

            def norm_T(self, xt, Rx, act_ok=True):
                self.norm_a(xt, Rx)
                self.norm_b()

            def norm_a(self, xt, Rx):
                st = self.st
                P.op("act", lambda e: e.activation(out=self.junk, in_=xt, func=ACTF.Square, accum_out=st[:, 0:1]),
                     reads=[Rx], writes=[self.Rxn, self.Rst])
                P.op("act", lambda e: e.activation(out=st[:, 1:2], in_=st[:, 0:1], func=ACTF.Ln, scale=1.0 / D, bias=epsc),
                     reads=[self.Rst, R_consts], pwrites=[self.Rst])
                P.op("act", lambda e: e.activation(out=st[:, 2:3], in_=st[:, 1:2], func=ACTF.Exp, scale=-0.5),
                     reads=[self.Rst], pwrites=[self.Rst])
                P.op("dve", lambda e: e.tensor_scalar(out=self.xn, in0=xt, scalar1=st[:, 2:3], scalar2=None, op0=ALU.mult),
                     reads=[Rx, self.Rst], writes=[self.Rxn])

            def norm_b(self):
                pT = pview(6, BF16, [8, 128])
                for c in range(8):
                    P.op("pe", lambda e, c=c: e.transpose(out=pT[:, c, :], in_=self.xn[:, c * 128:(c + 1) * 128], identity=ident_bf),
                         reads=[self.Rxn, R_consts], pwrites=[R_T])
                P.op("dve", lambda e: e.tensor_copy(out=self.xnT, in_=pT), reads=[R_T], writes=[self.RxnT])

            def final_norm_store(self, xt, Rx, gfin, Rg, row0, nrows, skey, junk, Rjunk):
                st = self.st
                P.op("act", lambda e: e.activation(out=junk, in_=xt, func=ACTF.Square, accum_out=st[:, 0:1]),
                     reads=[Rx], writes=[Rjunk, self.Rst])
                P.op("act", lambda e: e.activation(out=st[:, 1:2], in_=st[:, 0:1], func=ACTF.Ln, scale=1.0 / D, bias=epsc),
                     reads=[self.Rst, R_consts], pwrites=[self.Rst])
                P.op("act", lambda e: e.activation(out=st[:, 2:3], in_=st[:, 1:2], func=ACTF.Exp, scale=-0.5),
                     reads=[self.Rst], pwrites=[self.Rst])
                P.op("dve", lambda e: e.scalar_tensor_tensor(out=xt, in0=xt, scalar=st[:, 2:3], in1=gfin, op0=ALU.mult, op1=ALU.mult),
                     reads=[self.Rst, Rg, Rx], pwrites=[Rx])
                stores.append(P.dma("pool", skey, lambda e: e.dma_start(out=y_out.ap()[row0:row0 + nrows, :], in_=xt[0:nrows, :]),
                                    reads=[Rx]))

        def rstd_small(src, dst, tmp, n, Rs, width):
            P.op("act", lambda e: e.activation(out=tmp, in_=src, func=ACTF.Ln, scale=1.0 / width, bias=epsc),
                 reads=[Rs, R_consts], pwrites=[Rs])
            P.op("act", lambda e: e.activation(out=dst, in_=tmp, func=ACTF.Exp, scale=-0.5), reads=[Rs], pwrites=[Rs])

        def rope_seg(xin, xout, tabs, H, Rin, Rout, Rtab, tmp, Rtmp):
            cosb = tabs[:, 0:16].unsqueeze(1).unsqueeze(1).to_broadcast([128, H, 2, 16])
            nsin = tabs[:, 16:32].unsqueeze(1).to_broadcast([128, H, 16])
            psin = tabs[:, 32:48].unsqueeze(1).to_broadcast([128, H, 16])
            P.op("dve", lambda e: e.tensor_tensor(out=tmp[:, :, 0, :], in0=xin[:, :, 1, :], in1=nsin, op=ALU.mult),
                 reads=[Rin, Rtab], writes=[Rtmp])
            P.op("dve", lambda e: e.tensor_tensor(out=tmp[:, :, 1, :], in0=xin[:, :, 0, :], in1=psin, op=ALU.mult),
                 reads=[Rin, Rtab], pwrites=[Rtmp])
            P.op("dve", lambda e: e.tensor_tensor(out=xout, in0=xin, in1=cosb, op=ALU.mult), reads=[Rin, Rtab], pwrites=[Rout])
            P.op("dve", lambda e: e.tensor_tensor(out=xout, in0=xout, in1=tmp, op=ALU.add), reads=[Rtmp, Rout], pwrites=[Rout])

        def build_body():
            for si, S in enumerate(S_list):
                row_base = sum(S_list[:si])
                NT = S // 128
                NQB = S // 512
                for l in range(nlayers):
                    src_t = x_in if l == 0 else x1
                    dst_t = x1 if l < nlayers - 1 else None
                    lam_init = 0.8 - 0.6 * math.exp(-0.3 * l)
                    for grp in range(2):
                        barrier(prev_tail[0])
                        al = Alloc(PERS_END)
                        if grp == 0:
                            KT1 = al.take(BF16, [NT * 128])
                            KT2 = al.take(BF16, [4, NT * 128], 0, 96)
                            nv = 6
                        else:
                            KT1 = al.take(BF16, [NT * 128])
                            KT2 = al.take(BF16, [4, NT * 128], 0, 68)
                            nv = 6
                        Vst = al.take(BF16, [NT, nv, 65])
                        R_KT1, R_KT2, R_V = Res(), Res(), Res()
                        rope_a = al.take(F32, [NT, 48])
                        rope_b = al.take(F32, [48]) if grp == 1 else None
                        R_tab = Res()
                        gq = al.take(F32, [64])
                        gk = al.take(F32, [64])
                        gsub = al.take(F32, [64])
                        lamt = al.take(F32, [8])
                        lraw = al.take(F32, [128])
                        esink = al.take(F32, [4])
                        gcol = al.take(F32, [12])
                        bias_tab = al.take(BF16, [4, 384]) if grp == 0 else al.take(BF16, [4, 128])
                        R_sm = Res()
                        if grp == 0:
                            P.dma("sync", "c", lambda e: e.dma_start(out=rope_a, in_=c_rope1d.ap()[:, 0:NT, :]), pwrites=[R_tab])
                            P.dma("sync", "c", lambda e: e.dma_start(out=bias_tab, in_=c_stripA.ap()), pwrites=[R_tab])
                            P.dma("sync", "c", lambda e: e.dma_start(out=esink, in_=a_sink.ap()[l:l + 1, :].partition_broadcast(128)), writes=[R_sm])
                            P.op("act", lambda e: e.activation(out=esink, in_=esink, func=ACTF.Exp), reads=[R_sm], pwrites=[R_sm])
                        else:
                            P.dma("sync", "c", lambda e: e.dma_start(out=rope_a, in_=c_roperow.ap()[:, 0:NT, :]), pwrites=[R_tab])
                            P.dma("sync", "c", lambda e: e.dma_start(out=rope_b, in_=c_ropecol.ap()[:, 0, :]), pwrites=[R_tab])
                            P.dma("sync", "c", lambda e: e.dma_start(out=bias_tab, in_=c_toepD.ap()), pwrites=[R_tab])
                            P.dma("sync", "c", lambda e: e.dma_start(out=gq, in_=cqn.ap()[l:l + 1, :].partition_broadcast(128)), pwrites=[R_sm])
                            P.dma("sync", "c", lambda e: e.dma_start(out=gk, in_=ckn.ap()[l:l + 1, :].partition_broadcast(128)), pwrites=[R_sm])
                            P.dma("sync", "c", lambda e: e.dma_start(out=gsub, in_=dsn.ap()[l:l + 1, :].partition_broadcast(128)), pwrites=[R_sm])
                            P.dma("sync", "c", lambda e: e.dma_start(out=lraw, in_=lamv.ap()[l:l + 1, :].partition_broadcast(128)), pwrites=[R_sm])
                            for h in range(4):
                                P.dma("sync", "c", lambda e, h=h: e.dma_start(out=KT2[64:68, h, :], in_=c_kaugD.ap()[:, 0:S]), pwrites=[R_KT2])
                            lr = lraw.rearrange("p (a b) -> p a b", a=4)
                            P.op("dve", lambda e: e.tensor_tensor(out=lr[:, 0, :], in0=lr[:, 0, :], in1=lr[:, 1, :], op=ALU.mult), reads=[R_sm], pwrites=[R_sm])
                            P.op("dve", lambda e: e.tensor_tensor(out=lr[:, 2, :], in0=lr[:, 2, :], in1=lr[:, 3, :], op=ALU.mult), reads=[R_sm], pwrites=[R_sm])
                            P.op("dve", lambda e: e.tensor_reduce(out=lamt[:, 0:1], in_=lr[:, 0, :], axis=AX.X, op=ALU.add), reads=[R_sm], pwrites=[R_sm])
                            P.op("dve", lambda e: e.tensor_reduce(out=lamt[:, 1:2], in_=lr[:, 2, :], axis=AX.X, op=ALU.add), reads=[R_sm], pwrites=[R_sm])
                            P.op("act", lambda e: e.activation(out=lamt[:, 2:4], in_=lamt[:, 0:2], func=ACTF.Exp), reads=[R_sm], pwrites=[R_sm])
                            P.op("dve", lambda e: e.tensor_tensor(out=lamt[:, 4:5], in0=lamt[:, 3:4], in1=lamt[:, 2:3], op=ALU.subtract), reads=[R_sm], pwrites=[R_sm])
                            P.op("dve", lambda e: e.tensor_scalar(out=lamt[:, 5:6], in0=lamt[:, 4:5], scalar1=-lam_init, scalar2=None, op0=ALU.add), reads=[R_sm], pwrites=[R_sm])
                            P.op("dve", lambda e: e.tensor_scalar(out=gsub, in0=gsub, scalar1=(1.0 - lam_init), scalar2=None, op0=ALU.mult), reads=[R_sm], pwrites=[R_sm])
                        neglam = lamt[:, 5:6]
                        P.dma("sync", "c", lambda e: e.dma_start(out=gcol[:, 0:8], in_=g_attn_p.ap()[l]), pwrites=[R_sm])
                        P.dma("sync", "c", lambda e: e.dma_start(out=gcol[:, 8:10], in_=bqn_p.ap()[l]), pwrites=[R_sm])
                        P.dma("sync", "c", lambda e: e.dma_start(out=gcol[:, 10:11], in_=bkvn_p.ap()[l]), pwrites=[R_sm])
                        P.op("dve", lambda e: e.memset(Vst[:, :, :, 64:65], 1.0), pwrites=[R_V])
                        GRP_END = al.off
                        ckpt(11)

                        aw = Alloc(GRP_END)
                        wlq = WLoader(aw, dma_eng="pool", cast_engs=("pool",), key="wp")
                        wq = aw.take(BF16, [8, 512])
                        wo = aw.take(BF16, [4, D])
                        wqup = aw.take(BF16, [2, 384]) if grp == 0 else None
                        R_wq, R_wo = Res(), Res()
                        W_END = aw.off
                        a1 = Alloc(W_END)
                        wl = WLoader(a1, cast_engs=("dve",), nslots=6)
                        ncol_kv = 416 if grp == 0 else 768
                        wkv = a1.take(BF16, [8, ncol_kv])
                        R_wkv = Res()
                        if grp == 0:
                            kvsrc = [(1024 - 768, 0, 128), (1024 - 768 + 128, 128, 128)]
                        if grp == 0:
                            kvsrc = [(256, 0, 256), (768, 256, 160)]
                        else:
                            kvsrc = [(1184, 0, 256), (1696, 256, 512)]
                        for (sc, dc, n) in kvsrc:
                            for k in range(8):
                                wl.load(wkv[:, k, dc:dc + n], w_in.ap()[l, k * 128:(k + 1) * 128, sc:sc + n], n,
                                        scale=gcol[:, k:k + 1], Rdst=R_wkv, Rscale=R_sm)
                        if grp == 0:
                            wkvup = a1.take(BF16, [512])
                            wl.load(wkvup, w_kvup.ap()[l], 512, scale=gcol[:, 10:11], Rdst=R_wkv, Rscale=R_sm)
                        ckpt(12)
                        qsrc = [(0, 0, 256), (512, 256, 256)] if grp == 0 else [(928, 0, 256), (1440, 256, 256)]
                        for (sc, dc, n) in qsrc:
                            for k in range(8):
                                wlq.load(wq[:, k, dc:dc + n], w_in.ap()[l, k * 128:(k + 1) * 128, sc:sc + n], n,
                                         scale=gcol[:, k:k + 1], Rdst=R_wq, Rscale=R_sm)
                        for k in range(4):
                            for hf in range(2):
                                wlq.load(wo[:, k, hf * 512:(hf + 1) * 512],
                                         w_out.ap()[l, grp * 512 + k * 128: grp * 512 + (k + 1) * 128, hf * 512:(hf + 1) * 512], 512, Rdst=R_wo)
                        if grp == 0:
                            for k in range(2):
                                wlq.load(wqup[:, k, :], w_qup.ap()[l, k * 128:(k + 1) * 128, :], 384, scale=gcol[:, 8 + k:9 + k], Rdst=R_wq, Rscale=R_sm)
                        frs = [Front(a1, nx=1) for _ in range(2)]
                        stg_f2 = [a1.take(F32, [768]) for _ in range(2)]
                        stg_b2 = [a1.take(BF16, [512]) for _ in range(2)]
                        tmp_f2 = [a1.take(F32, [256]) for _ in range(2)]
                        sst_2 = [a1.take(F32, [16]) for _ in range(2)]
                        ckvT2 = [a1.take(BF16, [128]) for _ in range(2)]
                        R_stg2 = [Res(), Res()]
                        R_stgb2 = [Res(), Res()]
                        R_tmp2 = [Res(), Res()]
                        R_sst_2 = [Res(), Res()]
                        R_ckvT2 = [Res(), Res()]

                        def p1_gen(t, par):
                            fr = frs[par]
                            stg_f, stg_b, tmp_f, sst, ckvT = stg_f2[par], stg_b2[par], tmp_f2[par], sst_2[par], ckvT2[par]
                            R_stg, R_stgb, R_tmp, R_sst, R_ckvT = R_stg2[par], R_stgb2[par], R_tmp2[par], R_sst_2[par], R_ckvT2[par]
                            b0 = 2 * par
                            RS0, RS1 = R_S[b0], R_S[b0 + 1]
                            xt, Rx = fr.load(src_t, row_base + t * 128)
                            yield
                            fr.norm_a(xt, Rx)
                            yield
                            fr.norm_b()
                            yield
                            pk = pview(b0, F32, [ncol_kv])
                            for (c0, c1, bk) in ([(0, 416, 0)] if grp == 0 else [(0, 512, 0), (512, 768, 1)]):
                                for k in range(8):
                                    P.op("pe", lambda e: e.matmul(pk[:, c0:c1], lhsT=fr.xnT[:, k, :], rhs=wkv[:, k, c0:c1], start=(k == 0), stop=(k == 7)),
                                         reads=[fr.RxnT, R_wkv], pwrites=[R_S[b0 + bk]])
                            yield
                            tcol = slice(t * 128, (t + 1) * 128)
                            pT = pview(6, BF16, [8, 128])
                            if grp == 0:
                                P.op("act", lambda e: e.copy(out=stg_b[:, 0:128], in_=pk[:, 0:128]), reads=[RS0], writes=[R_stgb])
                                P.op("act", lambda e: e.copy(out=Vst[:, t, 0:2, 0:64], in_=pk[:, 128:256].rearrange("p (a b) -> p a b", a=2)),
                                     reads=[RS0], pwrites=[R_V])
                                P.op("dve", lambda e: e.tensor_copy(out=stg_f[:, 0:160], in_=pk[:, 256:416]), reads=[RS0], writes=[R_stg])
                                P.op("dve", lambda e: e.tensor_tensor(out=tmp_f[:, 0:128], in0=stg_f[:, 0:128], in1=stg_f[:, 0:128], op=ALU.mult),
                                     reads=[R_stg], writes=[R_tmp])
                                P.op("dve", lambda e: e.tensor_reduce(out=sst[:, 0:1], in_=tmp_f[:, 0:128], axis=AX.X, op=ALU.add),
                                     reads=[R_tmp], writes=[R_sst])
                                yield
                                P.op("pe", lambda e: e.transpose(out=pT[:, 0, :], in_=stg_b[:, 0:128], identity=ident_bf),
                                     reads=[R_stgb, R_consts], writes=[R_T])
                                P.op("dve", lambda e: e.tensor_copy(out=KT1[:, tcol], in_=pT[:, 0, :]), reads=[R_T], pwrites=[R_KT1])
                                rstd_small(sst[:, 0:1], sst[:, 2:3], sst[:, 1:2], 1, R_sst, 128.0)
                                P.op("dve", lambda e: e.tensor_scalar(out=stg_b[:, 128:256], in0=stg_f[:, 0:128], scalar1=sst[:, 2:3], scalar2=None, op0=ALU.mult),
                                     reads=[R_stg, R_sst], pwrites=[R_stgb])
                                yield
                                P.op("pe", lambda e: e.transpose(out=pT[:, 1, :], in_=stg_b[:, 128:256], identity=ident_bf),
                                     reads=[R_stgb, R_consts], writes=[R_T])
                                P.op("dve", lambda e: e.tensor_copy(out=ckvT, in_=pT[:, 1, :]), reads=[R_T], writes=[R_ckvT])
                                yield
                                pkv = pview(7, F32, [512])
                                P.op("pe", lambda e: e.matmul(pkv, lhsT=ckvT, rhs=wkvup, start=True, stop=True), reads=[R_ckvT, R_wkv], writes=[R_Y])
                                kcat = stg_b[:, 0:384].rearrange("p (a b) -> p a b", a=4)
                                pkv4 = pkv.rearrange("p (a b) -> p a b", a=4)
                                P.op("dve", lambda e: e.tensor_copy(out=kcat[:, :, 0:64], in_=pkv4[:, :, 0:64]), reads=[R_Y], writes=[R_stgb])
                                P.op("act", lambda e: e.copy(out=Vst[:, t, 2:6, 0:64], in_=pkv4[:, :, 64:128]), reads=[R_Y], pwrites=[R_V])
                                kr_in = stg_f[:, 128:160].rearrange("p (h a b) -> p h a b", h=1, a=2)
                                kr_out = tmp_f[:, 64:96].rearrange("p (h a b) -> p h a b", h=1, a=2)
                                kr_tmp = tmp_f[:, 128:160].rearrange("p (h a b) -> p h a b", h=1, a=2)
                                rope_seg(kr_in, kr_out, rope_a[:, t, :], 1, R_stg, R_tmp, R_tab, kr_tmp, R_tmp)
                                P.op("dve", lambda e: e.tensor_copy(out=kcat[:, :, 64:96], in_=tmp_f[:, 64:96].unsqueeze(1).to_broadcast([128, 4, 32])),
                                     reads=[R_tmp], pwrites=[R_stgb])
                                yield
                                for h in range(4):
                                    P.op("pe", lambda e: e.transpose(out=pT[0:96, 2 + h, :], in_=kcat[:, h, :], identity=ident_bf),
                                         reads=[R_stgb, R_consts], pwrites=[R_T])
                                P.op("dve", lambda e: e.tensor_copy(out=KT2[:, :, tcol], in_=pT[0:96, 2:6, :]), reads=[R_T], pwrites=[R_KT2])
                            else:
                                P.op("dve", lambda e: e.tensor_copy(out=stg_f[:, 0:128], in_=pk[:, 0:128]), reads=[RS0], writes=[R_stg])
                                P.op("act", lambda e: e.copy(out=Vst[:, t, 0:2, 0:64], in_=pk[:, 128:256].rearrange("p (a b) -> p a b", a=2)),
                                     reads=[RS0], pwrites=[R_V])
                                ck = stg_f[:, 0:128].rearrange("p (a b) -> p a b", a=2)
                                P.op("dve", lambda e: e.tensor_tensor(out=tmp_f[:, 0:128], in0=stg_f[:, 0:128], in1=stg_f[:, 0:128], op=ALU.mult),
                                     reads=[R_stg], writes=[R_tmp])
                                P.op("dve", lambda e: e.tensor_reduce(out=sst[:, 0:2], in_=tmp_f[:, 0:128].rearrange("p (a b) -> p a b", a=2), axis=AX.X, op=ALU.add),
                                     reads=[R_tmp], writes=[R_sst])
                                P.op("act", lambda e: e.copy(out=stg_b[:, 128:384], in_=pk[:, 256:512]), reads=[RS0], writes=[R_stgb])
                                P.op("act", lambda e: e.copy(out=Vst[:, t, 2:6, 0:64], in_=pk[:, 512:768].rearrange("p (a b) -> p a b", a=4)),
                                     reads=[RS1], pwrites=[R_V])
                                yield
                                for h in range(4):
                                    P.op("pe", lambda e: e.transpose(out=pT[0:64, 2 + h, :], in_=stg_b[:, 128 + 64 * h:192 + 64 * h], identity=ident_bf),
                                         reads=[R_stgb, R_consts], pwrites=[R_T])
                                P.op("dve", lambda e: e.tensor_copy(out=KT2[0:64, :, tcol], in_=pT[0:64, 2:6, :]), reads=[R_T], pwrites=[R_KT2])
                                rstd_small(sst[:, 0:2], sst[:, 4:6], sst[:, 2:4], 2, R_sst, 64.0)
                                P.op("dve", lambda e: e.tensor_tensor(out=ck, in0=ck, in1=sst[:, 4:6].unsqueeze(2).to_broadcast([128, 2, 64]), op=ALU.mult),
                                     reads=[R_sst, R_stg], pwrites=[R_stg])
                                P.op("dve", lambda e: e.tensor_tensor(out=ck, in0=ck, in1=gk.unsqueeze(1).to_broadcast([128, 2, 64]), op=ALU.mult),
                                     reads=[R_sm, R_stg], pwrites=[R_stg])
                                ck5 = stg_f[:, 0:128].rearrange("p (h s a b) -> p h s a b", h=2, s=2, a=2)
                                co5 = stg_f[:, 256:384].rearrange("p (h s a b) -> p h s a b", h=2, s=2, a=2)
                                tm5 = tmp_f[:, 0:128].rearrange("p (h s a b) -> p h s a b", h=2, s=2, a=2)
                                rope_seg(ck5[:, :, 0], co5[:, :, 0], rope_a[:, t, :], 2, R_stg, R_stg, R_tab, tm5[:, :, 0], R_tmp)
                                rope_seg(ck5[:, :, 1], co5[:, :, 1], rope_b, 2, R_stg, R_stg, R_tab, tm5[:, :, 1], R_tmp)
                                P.op("dve", lambda e: e.tensor_copy(out=stg_b[:, 0:128], in_=stg_f[:, 256:384]), reads=[R_stg], pwrites=[R_stgb])
                                yield
                                P.op("pe", lambda e: e.transpose(out=pT[:, 0, :], in_=stg_b[:, 0:128], identity=ident_bf),
                                     reads=[R_stgb, R_consts], writes=[R_T])
                                P.op("dve", lambda e: e.tensor_copy(out=KT1[:, tcol], in_=pT[:, 0, :]), reads=[R_T], pwrites=[R_KT1])

                        import collections as _co1
                        _g = _co1.deque()
                        _nt = 0
                        while _nt < NT or _g:
                            while len(_g) < 2 and _nt < NT:
                                _g.append(p1_gen(_nt, _nt % 2))
                                _nt += 1
                            g_ = _g.popleft()
                            try:
                                next(g_)
                                _g.append(g_)
                            except StopIteration:
                                pass
                        fr = frs[0]

                        if KSTOP == 1 + 2 * grp:
                            raise _Stop()
                        p1_tail = [R_wkv] + R_stg2 + R_stgb2 + R_tmp2 + R_sst_2 + R_ckvT2 + wl.R
                        for f_ in frs:
                            p1_tail += [f_.RxnT, f_.Rxn, f_.Rst] + f_.Rx
                        barrier(p1_tail)
                        a2 = Alloc(W_END)
                        wl = wlq
                        fr = Front(a2)
                        xres = [a2.take(F32, [D]) for _ in range(2)]
                        R_xres = [Res() for _ in range(2)]
                        stg_f = a2.take(F32, [768])
                        R_stg = Res()
                        stg_b = a2.take(BF16, [768])
                        R_stgb = Res()
                        tmp_f = a2.take(F32, [512])
                        R_tmp = Res()
                        sst = a2.take(F32, [16])
                        R_sst = Res()
                        cqT = a2.take(BF16, [2, 128])
                        R_cqT = Res()
                        if grp == 0:
                            Q1 = [a2.take(BF16, [4, 512]) for _ in range(2)]
                            Q2 = [a2.take(BF16, [4, 512], 0, 96) for _ in range(2)]
                        else:
                            Q1 = [a2.take(BF16, [4, 512]) for _ in range(2)]
                            Q2 = [a2.take(BF16, [4, 2, 2, 512], 0, 68) for _ in range(2)]
                        R_Q1 = [Res(), Res()]
                        R_Q2 = [Res(), Res()]
                        for b_ in range(2):
                            P.op("pool", lambda e, b_=b_: e.memset(Q1[b_], 0.0), writes=[R_Q1[b_]])
                        NPB = 4
                        PT = [a2.take(BF16, [2, 512]) for _ in range(NPB)]
                        R_PT = [Res() for _ in range(NPB)]
                        Osb = [a2.take(F32, [512], 0, 65) for _ in range(2)]
                        R_Osb = [Res() for _ in range(2)]
                        rz = a2.take(F32, [8])
                        R_rz = Res()
                        otm = [a2.take(BF16, [4, 512]) for _ in range(2)]
                        R_otm = [Res(), Res()]
                        od = a2.take(F32, [4, 2, 64])
                        R_od = Res()
                        dtmp = a2.take(F32, [4, 64])
                        R_dtmp = Res()
                        sst2 = a2.take(F32, [16])
                        R_sst2 = Res()
                        oT = a2.take(BF16, [4, 128])
                        R_oT = Res()
                        xo = [a2.take(F32, [D]) for _ in range(2)]
                        R_xo = [Res() for _ in range(2)]
                        if os.environ.get('KDEBUG'):
                            print('P2 alloc end', grp, S, a2.off, 'grp_end', GRP_END)
                        pt_i = [0]
                        o_i = [0]

                        def qprep_gen(qb, j):
                            qi = qb % 2
                            Q1c, Q2c, RQ1, RQ2 = Q1[qi], Q2[qi], R_Q1[qi], R_Q2[qi]
                            t = qb * 4 + j
                            if grp == 1 and j == 0:
                                q0 = qb * 512
                                P.dma("sync", "c", lambda e: e.dma_start(out=Q2c[64:68], in_=c_qaugD.ap()[:, :, :, :, q0:q0 + 512]), pwrites=[RQ2])
                            xt, Rx = fr.load(src_t, row_base + t * 128)
                            yield
                            fr.norm_a(xt, Rx)
                            yield
                            fr.norm_b()
                            yield
                            pq = pview(7, F32, [512])
                            for k in range(8):
                                P.op("pe", lambda e, k=k: e.matmul(pq, lhsT=fr.xnT[:, k, :], rhs=wq[:, k, :], start=(k == 0), stop=(k == 7)),
                                     reads=[fr.RxnT, R_wq], pwrites=[R_Y])
                            yield
                            jc = slice(j * 128, (j + 1) * 128)
                            pT = pview(6, BF16, [8, 128])
                            if grp == 0:
                                aq = stg_b[:, 0:256].rearrange("p (i g d) -> p i g d", i=2, g=2)
                                pqa = pq[:, 0:256].rearrange("p (g i d) -> p i g d", g=2, i=2)
                                P.op("dve", lambda e: e.tensor_scalar(out=aq, in0=pqa, scalar1=0.125, scalar2=None, op0=ALU.mult),
                                     reads=[R_Y], writes=[R_stgb])
                                P.op("dve", lambda e: e.tensor_copy(out=stg_f[:, 0:256], in_=pq[:, 256:512]), reads=[R_Y], writes=[R_stg])
                                P.op("dve", lambda e: e.tensor_tensor(out=tmp_f[:, 0:256], in0=stg_f[:, 0:256], in1=stg_f[:, 0:256], op=ALU.mult),
                                     reads=[R_stg], writes=[R_tmp])
                                P.op("dve", lambda e: e.tensor_reduce(out=sst[:, 0:1], in_=tmp_f[:, 0:256], axis=AX.X, op=ALU.add),
                                     reads=[R_tmp], writes=[R_sst])
                                yield
                                for i in range(2):
                                    P.op("pe", lambda e, i=i: e.transpose(out=pT[:, i, :], in_=stg_b[:, i * 128:(i + 1) * 128], identity=ident_bf),
                                         reads=[R_stgb, R_consts], pwrites=[R_T])
                                P.op("dve", lambda e: e.tensor_copy(out=Q1c[0:64, 0:2, jc], in_=pT[0:64, 0:2, :]), reads=[R_T], pwrites=[RQ1])
                                P.op("dve", lambda e: e.tensor_copy(out=Q1c[64:128, 2:4, jc], in_=pT[64:128, 0:2, :]), reads=[R_T], pwrites=[RQ1])
                                rstd_small(sst[:, 0:1], sst[:, 2:3], sst[:, 1:2], 1, R_sst, 256.0)
                                P.op("dve", lambda e: e.tensor_scalar(out=stg_b[:, 256:512], in0=stg_f[:, 0:256], scalar1=sst[:, 2:3], scalar2=None, op0=ALU.mult),
                                     reads=[R_stg, R_sst], pwrites=[R_stgb])
                                yield
                                for i in range(2):
                                    P.op("pe", lambda e, i=i: e.transpose(out=pT[:, 2 + i, :], in_=stg_b[:, 256 + i * 128:384 + i * 128], identity=ident_bf),
                                         reads=[R_stgb, R_consts], pwrites=[R_T])
                                P.op("dve", lambda e: e.tensor_copy(out=cqT, in_=pT[:, 2:4, :]), reads=[R_T], writes=[R_cqT])
                                yield
                                pqq = pview(7, F32, [384])
                                for i in range(2):
                                    P.op("pe", lambda e, i=i: e.matmul(pqq, lhsT=cqT[:, i, :], rhs=wqup[:, i, :], start=(i == 0), stop=(i == 1)),
                                         reads=[R_cqT, R_wq], pwrites=[R_Y])
                                yield
                                sc_b = 96.0 ** -0.5
                                P.op("dve", lambda e: e.tensor_scalar(out=stg_f[:, 256:640], in0=pqq, scalar1=sc_b, scalar2=None, op0=ALU.mult),
                                     reads=[R_Y], pwrites=[R_stg])
                                q4 = stg_f[:, 256:640].rearrange("p (h d) -> p h d", h=4)
                                qr_in = q4[:, :, 64:96].rearrange("p h (a b) -> p h a b", a=2)
                                qr_out = tmp_f[:, 0:128].rearrange("p (h a b) -> p h a b", h=4, a=2)
                                qr_tmp = tmp_f[:, 128:256].rearrange("p (h a b) -> p h a b", h=4, a=2)
                                rope_seg(qr_in, qr_out, rope_a[:, t, :], 4, R_stg, R_tmp, R_tab, qr_tmp, R_tmp)
                                qcat = stg_b[:, 0:384].rearrange("p (h d) -> p h d", h=4)
                                P.op("dve", lambda e: e.tensor_copy(out=qcat[:, :, 0:64], in_=q4[:, :, 0:64]), reads=[R_stg], writes=[R_stgb])
                                P.op("dve", lambda e: e.tensor_copy(out=qcat[:, :, 64:96], in_=tmp_f[:, 0:128].rearrange("p (h d) -> p h d", h=4)),
                                     reads=[R_tmp], pwrites=[R_stgb])
                                yield
                                for h in range(4):
                                    P.op("pe", lambda e, h=h: e.transpose(out=pT[0:96, 4 + h, :], in_=qcat[:, h, :], identity=ident_bf),
                                         reads=[R_stgb, R_consts], pwrites=[R_T])
                                P.op("dve", lambda e: e.tensor_copy(out=Q2c[:, :, jc], in_=pT[0:96, 4:8, :]), reads=[R_T], pwrites=[RQ2])
                            else:
                                P.op("dve", lambda e: e.tensor_copy(out=stg_f[:, 0:256], in_=pq[:, 0:256]), reads=[R_Y], writes=[R_stg])
                                cq4 = stg_f[:, 0:256].rearrange("p (a b) -> p a b", a=4)
                                P.op("dve", lambda e: e.tensor_tensor(out=tmp_f[:, 0:256], in0=stg_f[:, 0:256], in1=stg_f[:, 0:256], op=ALU.mult),
                                     reads=[R_stg], writes=[R_tmp])
                                P.op("dve", lambda e: e.tensor_reduce(out=sst[:, 0:4], in_=tmp_f[:, 0:256].rearrange("p (a b) -> p a b", a=4), axis=AX.X, op=ALU.add),
                                     reads=[R_tmp], writes=[R_sst])
                                sc_d = 32.0 ** -0.5
                                dq = stg_b[:, 256:768].rearrange("p (h c d) -> p h c d", h=4, c=2)
                                P.op("pool", lambda e: e.memset(stg_b[:, 256:768], 0.0), writes=[R_stgb])
                                pqd = pq[:, 256:512].rearrange("p (h c d) -> p h c d", h=4, c=2)
                                for cc in range(2):
                                    P.op("dve", lambda e, cc=cc: e.tensor_scalar(out=dq[:, :, cc, 32 * cc:32 * cc + 32], in0=pqd[:, :, cc, :], scalar1=sc_d, scalar2=None, op0=ALU.mult),
                                         reads=[R_Y], pwrites=[R_stgb])
                                yield
                                pT8 = pview(6, BF16, [8, 128])
                                for h in range(4):
                                    for cc in range(2):
                                        P.op("pe", lambda e, h=h, cc=cc: e.transpose(out=pT8[0:64, h * 2 + cc, :], in_=dq[:, h, cc, :], identity=ident_bf),
                                             reads=[R_stgb, R_consts], pwrites=[R_T])
                                pT8v = pT8[0:64, :, :].rearrange("p (h c) t -> p h c t", h=4)
                                for v in range(2):
                                    P.op("dve", lambda e, v=v: e.tensor_copy(out=Q2c[0:64, :, :, v, jc], in_=pT8v), reads=[R_T], pwrites=[RQ2])
                                rstd_small(sst[:, 0:4], sst[:, 8:12], sst[:, 4:8], 4, R_sst, 64.0)
                                P.op("dve", lambda e: e.tensor_tensor(out=cq4, in0=cq4, in1=sst[:, 8:12].unsqueeze(2).to_broadcast([128, 4, 64]), op=ALU.mult),
                                     reads=[R_sst, R_stg], pwrites=[R_stg])
                                P.op("dve", lambda e: e.scalar_tensor_tensor(out=stg_f[:, 0:256].rearrange("p (a b) -> p a b", a=4), in0=cq4, scalar=0.125,
                                                                            in1=gq.unsqueeze(1).to_broadcast([128, 4, 64]), op0=ALU.mult, op1=ALU.mult),
                                     reads=[R_sm, R_stg], pwrites=[R_stg])
                                cq5 = stg_f[:, 0:256].rearrange("p (h s a b) -> p h s a b", h=4, s=2, a=2)
                                co5 = stg_f[:, 256:512].rearrange("p (h s a b) -> p h s a b", h=4, s=2, a=2)
                                tm5 = tmp_f[:, 0:256].rearrange("p (h s a b) -> p h s a b", h=4, s=2, a=2)
                                rope_seg(cq5[:, :, 0], co5[:, :, 0], rope_a[:, t, :], 4, R_stg, R_stg, R_tab, tm5[:, :, 0], R_tmp)
                                rope_seg(cq5[:, :, 1], co5[:, :, 1], rope_b, 4, R_stg, R_stg, R_tab, tm5[:, :, 1], R_tmp)
                                P.op("dve", lambda e: e.tensor_copy(out=stg_b[:, 0:256].rearrange("p (i g d) -> p i g d", i=2, g=2),
                                                                   in_=stg_f[:, 256:512].rearrange("p (g i d) -> p i g d", g=2, i=2)),
                                     reads=[R_stg], writes=[R_stgb])
                                yield
                                for i in range(2):
                                    P.op("pe", lambda e, i=i: e.transpose(out=pT[:, i, :], in_=stg_b[:, i * 128:(i + 1) * 128], identity=ident_bf),
                                         reads=[R_stgb, R_consts], pwrites=[R_T])
                                P.op("dve", lambda e: e.tensor_copy(out=Q1c[0:64, 0:2, jc], in_=pT[0:64, 0:2, :]), reads=[R_T], pwrites=[RQ1])
                                P.op("dve", lambda e: e.tensor_copy(out=Q1c[64:128, 2:4, jc], in_=pT[64:128, 0:2, :]), reads=[R_T], pwrites=[RQ1])

                        def outproj_gen(qb, j):
                            t = qb * 4 + j
                            row0 = row_base + t * 128
                            otc, Rotc = otm[qb % 2], R_otm[qb % 2]
                            xr = xres[j % 2]
                            Rxr = R_xres[j % 2]
                            rsrc = src_t if grp == 0 else xmid
                            rr = [hres(rsrc, row0)] if rsrc is not x_in else []
                            P.dma("sync", "xr%d" % (j % 2), lambda e: e.dma_start(out=xr, in_=rsrc.ap()[row0:row0 + 128, :]), reads=rr, writes=[Rxr])
                            yield
                            pOT = pview(6, BF16, [8, 128])
                            for c in range(4):
                                P.op("pe", lambda e, c=c: e.transpose(out=pOT[:, c, :], in_=otc[:, j, c * 128:(c + 1) * 128], identity=ident_bf),
                                     reads=[Rotc, R_consts], pwrites=[R_T])
                            P.op("dve", lambda e: e.tensor_copy(out=oT, in_=pOT[:, 0:4, :]), reads=[R_T], writes=[R_oT])
                            xw = xo[j % 2]
                            Rxw = R_xo[j % 2]
                            for hf in range(2):
                                yield
                                pY = pview(7, F32, [512])
                                for c in range(4):
                                    P.op("pe", lambda e, c=c, hf=hf: e.matmul(pY, lhsT=oT[:, c, :], rhs=wo[:, c, hf * 512:(hf + 1) * 512], start=(c == 0), stop=(c == 3)),
                                         reads=[R_oT, R_wo], pwrites=[R_Y])
                                P.op("dve", lambda e, hf=hf: e.tensor_tensor(out=xw[:, hf * 512:(hf + 1) * 512], in0=pY, in1=xr[:, hf * 512:(hf + 1) * 512], op=ALU.add),
                                     reads=[R_Y, Rxr], **({"writes": [Rxw]} if hf == 0 else {"pwrites": [Rxw]}))
                            P.dma("pool", "so%d" % (j % 2), lambda e: e.dma_start(out=xmid.ap()[row0:row0 + 128, :], in_=xw), reads=[Rxw], writes=[hres(xmid, row0)])

                        import collections as _co
                        tasks = _co.deque()
                        ntasks = _co.deque()

                        def _step(dq):
                            while dq:
                                try:
                                    next(dq[0])
                                    return True
                                except StopIteration:
                                    dq.popleft()
                            return False

                        def hook():
                            if _step(ntasks):
                                return
                            _step(tasks)

                        def flush_ntasks():
                            while ntasks:
                                for _ in ntasks.popleft():
                                    pass

                        def flush_tasks():
                            flush_ntasks()
                            while tasks:
                                for _ in tasks.popleft():
                                    pass

                        for j in range(4):
                            tasks.append(qprep_gen(0, j))
                        flush_tasks()

                        def norm_gen(c):
                            kind, h, ob, osb, otc, Rotc = c["kind"], c["h"], c["ob"], c["osb"], c["otc"], c["Rotc"]
                            pN = pview(6, F32, [4, 128])
                            for jj in range(4):
                                P.op("pe", lambda e, jj=jj: e.transpose(out=pN[:, jj, 0:65], in_=osb[:, jj * 128:(jj + 1) * 128], identity=ident_f[0:65, 0:65]),
                                     reads=[R_Osb[ob], R_consts], pwrites=[R_T])
                            P.op("dve", lambda e: e.reciprocal(out=rz[:, 0:4], in_=pN[:, :, 64]), reads=[R_T], writes=[R_rz])
                            rzb = rz[:, 0:4].unsqueeze(2).to_broadcast([128, 4, 64])
                            if kind != "D":
                                slot_f = (0 if kind in ("A", "C") else 256) + 64 * h
                                P.op("dve", lambda e: e.tensor_tensor(out=otc[:, :, slot_f:slot_f + 64], in0=pN[:, :, 0:64], in1=rzb, op=ALU.mult),
                                     reads=[R_T, R_rz], pwrites=[Rotc])
                                return
                            cc = c["cc"]
                            P.op("dve", lambda e: e.tensor_tensor(out=od[:, :, cc, :], in0=pN[:, :, 0:64], in1=rzb, op=ALU.mult),
                                 reads=[R_T, R_rz], **({"writes": [R_od]} if cc == 0 else {"pwrites": [R_od]}))
                            if cc == 0:
                                return
                            P.op("dve", lambda e: e.scalar_tensor_tensor(out=od[:, :, 0, :], in0=od[:, :, 1, :], scalar=neglam, in1=od[:, :, 0, :], op0=ALU.mult, op1=ALU.add),
                                 reads=[R_sm, R_od], pwrites=[R_od])
                            P.op("dve", lambda e: e.tensor_tensor(out=dtmp, in0=od[:, :, 0, :], in1=od[:, :, 0, :], op=ALU.mult), reads=[R_od], writes=[R_dtmp])
                            P.op("dve", lambda e: e.tensor_reduce(out=sst2[:, 0:4], in_=dtmp, axis=AX.X, op=ALU.add), reads=[R_dtmp], writes=[R_sst2])
                            yield
                            rstd_small(sst2[:, 0:4], sst2[:, 8:12], sst2[:, 4:8], 4, R_sst2, 64.0)
                            P.op("dve", lambda e: e.tensor_tensor(out=dtmp, in0=od[:, :, 0, :], in1=sst2[:, 8:12].unsqueeze(2).to_broadcast([128, 4, 64]), op=ALU.mult),
                                 reads=[R_od, R_sst2], writes=[R_dtmp])
                            P.op("dve", lambda e: e.tensor_tensor(out=otc[:, :, 256 + 64 * h:320 + 64 * h], in0=dtmp, in1=gsub.unsqueeze(1).to_broadcast([128, 4, 64]), op=ALU.mult),
                                 reads=[R_dtmp, R_sm], pwrites=[Rotc])

                        def emit_qk(c, unit, slot):
                            kind, h, qb = c["kind"], c["h"], c["qb"]
                            kt_ap, q_ap, RK, RQ, Q2c = c["kt_ap"], c["q_ap"], c["RK"], c["RQ"], c["Q2c"]
                            for ui, kt in enumerate(unit):
                                bank = slot * 2 + ui
                                pS = pview(bank, F32, [512])
                                tt = kt - 4 * qb
                                if kind in ("B", "C"):
                                    P.op("pe", lambda e: e.matmul(pS, lhsT=kt_ap(kt), rhs=q_ap(0, 512), start=True, stop=True),
                                         reads=[RK, RQ], writes=[R_S[bank]])
                                elif kind == "A":
                                    c0, c1 = max(0, 128 * (tt - 1)), min(512, 128 * (tt + 2))
                                    u0 = c0 - 128 * tt + 128
                                    P.op("pe", lambda e: e.matmul(pS[:, c0:c1], lhsT=kt_ap(kt), rhs=q_ap(c0, c1), start=True, stop=False, skip_group_check=True),
                                         reads=[RK, RQ], writes=[R_S[bank]])
                                    P.op("pe", lambda e: e.matmul(pS[:, c0:c1], lhsT=ident_bf, rhs=bias_tab[:, h, u0:u0 + (c1 - c0)], start=False, stop=True, skip_group_check=True),
                                         reads=[R_tab, R_consts], pwrites=[R_S[bank]])
                                else:
                                    cc = c["cc"]
                                    ks = slice(kt * 128, (kt + 1) * 128)
                                    if tt < 0 or tt > 3:
                                        var = 0 if tt < 0 else 1
                                        P.op("pe", lambda e: e.matmul(pS, lhsT=KT2[0:68, h, ks], rhs=Q2c[0:68, h, cc, var, 0:512], start=True, stop=True),
                                             reads=[RK, RQ], writes=[R_S[bank]])
                                    else:
                                        firstm = True
                                        for jj in range(4):
                                            cs = slice(128 * jj, 128 * jj + 128)
                                            wkw = {"writes": [R_S[bank]]} if firstm else {"pwrites": [R_S[bank]]}
                                            if jj == tt:
                                                P.op("pe", lambda e: e.matmul(pS[:, cs], lhsT=KT2[0:64, h, ks], rhs=Q2c[0:64, h, cc, 0, cs], start=firstm, stop=False, skip_group_check=True),
                                                     reads=[RK, RQ], **wkw)
                                                P.op("pe", lambda e: e.matmul(pS[:, cs], lhsT=ident_bf, rhs=bias_tab[:, h, :], start=False, stop=True, skip_group_check=True),
                                                     reads=[R_tab, R_consts], pwrites=[R_S[bank]])
                                            else:
                                                var = 1 if jj < tt else 0
                                                P.op("pe", lambda e: e.matmul(pS[:, cs], lhsT=KT2[0:68, h, ks], rhs=Q2c[0:68, h, cc, var, cs], start=firstm, stop=(jj == 3), skip_group_check=True),
                                                     reads=[RK, RQ], **wkw)
                                            firstm = False

                        def emit_exp(c, unit, slot):
                            kind, qb = c["kind"], c["qb"]
                            pb = pt_i[0] % NPB
                            pt_i[0] += 1
                            if kind == "A":
                                for ui, kt in enumerate(unit):
                                    bank = slot * 2 + ui
                                    tt = kt - 4 * qb
                                    c0, c1 = max(0, 128 * (tt - 1)), min(512, 128 * (tt + 2))
                                    pS = pview(bank, F32, [512])
                                    P.op("act", lambda e: e.activation(out=PT[pb][:, ui, c0:c1], in_=pS[:, c0:c1], func=ACTF.Exp),
                                         reads=[R_S[bank]], **({"writes": [R_PT[pb]]} if ui == 0 else {"pwrites": [R_PT[pb]]}))
                            else:
                                n = len(unit)
                                pS2 = pview(slot * 2, F32, [n, 512])
                                P.op("act", lambda e: e.activation(out=PT[pb][:, 0:n, :], in_=pS2, func=ACTF.Exp),
                                     reads=[R_S[slot * 2 + a] for a in range(n)], writes=[R_PT[pb]])
                            return pb

                        def emit_pv(c, unit, pb, first_unit, last_unit):
                            kind, qb, ob, v_ap = c["kind"], c["qb"], c["ob"], c["v_ap"]
                            pO = c["pO"]
                            for ui, kt in enumerate(unit):
                                st = first_unit and ui == 0
                                sp = last_unit and ui == len(unit) - 1
                                if kind == "A":
                                    tt = kt - 4 * qb
                                    c0, c1 = max(0, 128 * (tt - 1)), min(512, 128 * (tt + 2))
                                else:
                                    c0, c1 = 0, 512
                                P.op("pe", lambda e: e.matmul(pO[:, c0:c1], lhsT=v_ap(kt), rhs=PT[pb][:, ui, c0:c1], start=st, stop=sp, skip_group_check=True),
                                     reads=[R_V, R_PT[pb]], **({"writes": [R_O[ob]]} if st else {"pwrites": [R_O[ob]]}))
                            if last_unit:
                                osb, h = c["osb"], c["h"]
                                while len(ntasks) >= 2:
                                    for _ in ntasks.popleft():
                                        pass
                                P.op("dve", lambda e: e.tensor_copy(out=osb, in_=pO), reads=[R_O[ob]], writes=[R_Osb[ob]])
                                if kind == "A":
                                    P.op("dve", lambda e: e.tensor_scalar(out=osb[64:65, :], in0=osb[64:65, :], scalar1=esink[64:65, h:h + 1], scalar2=None, op0=ALU.add),
                                         reads=[R_sm, R_Osb[ob]], pwrites=[R_Osb[ob]])
                                ntasks.append(norm_gen(c))

                        def make_ctx(mp, qb):
                            qi = qb % 2
                            Q1c, Q2c, RQ1, RQ2 = Q1[qi], Q2[qi], R_Q1[qi], R_Q2[qi]
                            kind, h = mp[0], mp[1]
                            ob = o_i[0] % 2
                            o_i[0] += 1
                            c = {"kind": kind, "h": h, "qb": qb, "ob": ob, "pO": pview(4 + ob, F32, [512], 0, 65), "osb": Osb[ob],
                                 "otc": otm[qi], "Rotc": R_otm[qi], "Q2c": Q2c, "cc": (mp[2] if kind == "D" else 0)}
                            if kind in ("A", "C"):
                                g = h // 2
                                c.update(kt_ap=lambda kt: KT1[:, kt * 128:(kt + 1) * 128], q_ap=lambda c0, c1: Q1c[:, h, c0:c1],
                                         v_ap=lambda kt: Vst[:, kt, g, :], RK=R_KT1, RQ=RQ1)
                            elif kind == "B":
                                c.update(kt_ap=lambda kt: KT2[:, h, kt * 128:(kt + 1) * 128], q_ap=lambda c0, c1: Q2c[:, h, c0:c1],
                                         v_ap=lambda kt: Vst[:, kt, 2 + h, :], RK=R_KT2, RQ=RQ2)
                            else:
                                c.update(kt_ap=None, q_ap=None, v_ap=lambda kt: Vst[:, kt, 2 + h, :], RK=R_KT2, RQ=RQ2)
                            if kind == "A":
                                kts = [kt for kt in range(4 * qb - 1, 4 * qb + 5) if 0 <= kt < NT]
                            else:
                                kts = list(range(NT))
                            c["units"] = [kts[a:a + 2] for a in range(0, len(kts), 2)]
                            return c

                        for qb in range(NQB):
                            if qb + 1 < NQB:
                                for j in range(4):
                                    tasks.append(qprep_gen(qb + 1, j))
                            if grp == 0:
                                maps = [("A", h) for h in range(4)] + [("B", h) for h in range(4)]
                            else:
                                maps = [("C", h) for h in range(4)] + [("D", h, cc) for h in range(4) for cc in range(2)]
                            pipe = _co.deque()
                            gi = 0
                            for mp in maps:
                                c = make_ctx(mp, qb)
                                nu = len(c["units"])
                                for ui_, unit in enumerate(c["units"]):
                                    slot = gi % 2
                                    emit_qk(c, unit, slot)
                                    pb = emit_exp(c, unit, slot)
                                    pipe.append((c, unit, pb, ui_ == 0, ui_ == nu - 1))
                                    if len(pipe) > 2:
                                        emit_pv(*pipe.popleft())
                                    if gi % 2 == 1 or NT <= 16:
                                        hook()
                                    gi += 1
                            while pipe:
                                emit_pv(*pipe.popleft())
                            flush_tasks()
                            for j in range(4):
                                tasks.append(outproj_gen(qb, j))
                            if qb + 1 < NQB:
                                pass
                            else:
                                flush_tasks()

                        grp_tail = [R_KT1, R_KT2, R_V, R_tab, R_sm, R_wq, R_wo, fr.RxnT, fr.Rxn, fr.Rjunk, fr.Rst, R_stg, R_stgb, R_tmp, R_sst, R_cqT,
                                    R_rz, R_od, R_dtmp, R_oT, R_sst2] + R_Q1 + R_Q2 + R_otm + fr.Rx + wl.R + R_xres + R_PT + R_Osb + R_xo
                        prev_tail[0] = grp_tail
                        if KSTOP == 2 + 2 * grp:
                            raise _Stop()

                    barrier(prev_tail[0])
                    af = Alloc(PERS_END)
                    gT = af.take(BF16, [NFC, 512])
                    _go = af.last // 4
                    wl = WLoader(af, cast_engs=("dve", "act", "pool"), slots=[arena[0:128, _go + 512 * i_: _go + 512 * (i_ + 1)] for i_ in range(8)])
                    wu = af.take(BF16, [8, 2 * DFF])
                    wd = af.take(BF16, [NFC, D])
                    R_wu, R_wd = Res(), Res()
                    R_fs = Res()
                    gcol = af.take(F32, [8])
                    bup = af.take(F32, [44])
                    cw = af.take(F32, [3, 44])
                    cb = af.take(F32, [44])
                    b1 = af.take(F32, [44])
                    gfin = af.take(F32, [D])
                    P.dma("sync", "c", lambda e: e.dma_start(out=gcol, in_=g_ffn_p.ap()[l]), pwrites=[R_fs])
                    P.dma("sync", "c", lambda e: e.dma_start(out=bup, in_=b_up_p.ap()[l]), pwrites=[R_fs])
                    P.dma("sync", "c", lambda e: e.dma_start(out=cw, in_=conv_w_p.ap()[l].rearrange("p (a b) -> p a b", a=3)), pwrites=[R_fs])
                    P.dma("sync", "c", lambda e: e.dma_start(out=cb, in_=conv_b_p.ap()[l]), pwrites=[R_fs])
                    P.dma("sync", "c", lambda e: e.dma_start(out=gfin, in_=g_final.ap().partition_broadcast(128)), pwrites=[R_fs])
                    P.op("dve", lambda e: e.tensor_tensor(out=b1, in0=bup, in1=cw[:, 1, :], op=ALU.mult), reads=[R_fs], pwrites=[R_fs])
                    P.op("dve", lambda e: e.tensor_tensor(out=b1, in0=b1, in1=cb, op=ALU.add), reads=[R_fs], pwrites=[R_fs])
                    for k in range(8):
                        for c0 in range(0, 2 * DFF, 512):
                            n = min(512, 2 * DFF - c0)
                            wl.load(wu[:, k, c0:c0 + n], w_up.ap()[l, k * 128:(k + 1) * 128, c0:c0 + n], n, scale=gcol[:, k:k + 1], Rdst=R_wu, Rscale=R_fs)
                    for c in range(NFC):
                        for hf in range(2):
                            wl.load(wd[:, c, hf * 512:(hf + 1) * 512], w_down.ap()[l, c * 128:(c + 1) * 128, hf * 512:(hf + 1) * 512], 512, Rdst=R_wd)
                    fr = Front(af, nx=1)
                    x2T_b = [af.take(BF16, [8, 512]) for _ in range(2)]
                    R_x2T_b = [Res(), Res()]
                    hb = [af.take(F32, [512]) for _ in range(2)]
                    R_hb = [Res() for _ in range(2)]
                    t1 = [af.take(F32, [512]) for _ in range(2)]
                    R_t1 = [Res() for _ in range(2)]
                    sa = af.take(F32, [512])
                    sa_bf = arena_bf[0:128, af.last // 2: af.last // 2 + 1024]
                    R_sa = Res()
                    R_gT = Res()
                    P.op("dve", lambda e: e.memset(gT[:, 0, 0:2], 0.0), writes=wl.R + [R_gT])
                    xr2 = [af.take(F32, [D]) for _ in range(2)]
                    R_xr2 = [Res() for _ in range(2)]

                    if os.environ.get('KDEBUG'):
                        print('FFN alloc end', S, af.off)
                    ob_ = 0
                    blocks = []
                    blk0 = 0
                    while blk0 < S:
                        nout = min(510, S - blk0)
                        blocks.append((blk0, nout))
                        blk0 += nout

                    def x2T_gen(bi):
                        blk0, nout = blocks[bi]
                        x2T, R_x2T = x2T_b[bi % 2], R_x2T_b[bi % 2]
                        lo = blk0 - 1
                        ncol = nout + 2
                        ci = 0
                        while ci < ncol:
                            tok = lo + ci
                            if tok < 0:
                                P.op("dve", lambda e: e.memset(x2T[:, :, ci:ci + 1], 0.0), pwrites=[R_x2T])
                                ci += 1
                                continue
                            nr = min(128, ncol - ci, S - tok)
                            if nr <= 0:
                                P.op("dve", lambda e: e.memset(x2T[:, :, ci:ncol], 0.0), pwrites=[R_x2T])
                                break
                            xt, Rx = fr.load(xmid, row_base + tok, nr)
                            yield
                            fr.norm_a(xt, Rx)
                            yield
                            fr.norm_b()
                            P.op("dve", lambda e: e.tensor_copy(out=x2T[:, :, ci:ci + nr], in_=fr.xnT[:, :, 0:nr]), reads=[fr.RxnT], pwrites=[R_x2T])
                            yield
                            ci += nr

                    import collections as _co2
                    ftasks = _co2.deque()

                    def fstep():
                        while ftasks:
                            try:
                                next(ftasks[0])
                                return
                            except StopIteration:
                                ftasks.popleft()

                    def fflush():
                        while ftasks:
                            for _ in ftasks.popleft():
                                pass

                    ftasks.append(x2T_gen(0))
                    fflush()
                    for bi, (blk0, nout) in enumerate(blocks):
                        x2T, R_x2T = x2T_b[bi % 2], R_x2T_b[bi % 2]
                        if bi + 1 < len(blocks):
                            ftasks.append(x2T_gen(bi + 1))
                        lo = blk0 - 1
                        ncol = nout + 2
                        pad_lo = (lo < 0)
                        pad_hi = (lo + ncol > S)
                        for i in range(NFC):
                            for half in range(2):
                                fc = i + NFC * half
                                bank = (2 * i + half) % 4
                                pU = pview(bank, F32, [512])
                                for k in range(8):
                                    P.op("pe", lambda e, k=k, fc=fc, pU=pU, ncol=ncol: e.matmul(pU[:, 0:ncol], lhsT=wu[:, k, fc * 128:(fc + 1) * 128], rhs=x2T[:, k, 0:ncol],
                                                                                              start=(k == 0), stop=(k == 7)),
                                         reads=[R_wu, R_x2T], pwrites=[R_S[bank]])
                                hbt, Rh = hb[half], R_hb[half]
                                t1t, Rt = t1[half], R_t1[half]
                                P.op("act", lambda e, hbt=hbt, pU=pU, fc=fc, ncol=ncol: e.activation(out=hbt[:, 0:ncol], in_=pU[:, 0:ncol], func=ACTF.Identity, bias=bup[:, fc:fc + 1]),
                                     reads=[R_S[bank], R_fs], writes=[Rh])
                                if pad_lo:
                                    P.op("pool", lambda e, hbt=hbt: e.memset(hbt[:, 0:1], 0.0), pwrites=[Rh])
                                if pad_hi:
                                    P.op("pool", lambda e, hbt=hbt, ncol=ncol: e.memset(hbt[:, ncol - 1:ncol], 0.0), pwrites=[Rh])
                                P.op("act", lambda e, t1t=t1t, pU=pU, fc=fc, nout=nout: e.activation(out=t1t[:, 0:nout], in_=pU[:, 1:1 + nout], func=ACTF.Identity,
                                                                                                     bias=b1[:, fc:fc + 1], scale=cw[:, 1, fc:fc + 1]),
                                     reads=[R_S[bank], R_fs], writes=[Rt])
                                P.op("dve", lambda e, t1t=t1t, hbt=hbt, fc=fc, nout=nout: e.scalar_tensor_tensor(out=t1t[:, 0:nout], in0=hbt[:, 0:nout], scalar=cw[:, 0, fc:fc + 1],
                                                                                                                 in1=t1t[:, 0:nout], op0=ALU.mult, op1=ALU.add),
                                     reads=[Rh, R_fs, Rt], pwrites=[Rt])
                                P.op("dve", lambda e, t1t=t1t, hbt=hbt, fc=fc, nout=nout: e.scalar_tensor_tensor(out=t1t[:, 0:nout], in0=hbt[:, 2:2 + nout], scalar=cw[:, 2, fc:fc + 1],
                                                                                                                 in1=t1t[:, 0:nout], op0=ALU.mult, op1=ALU.add),
                                     reads=[Rh, R_fs, Rt], pwrites=[Rt])
                            P.op("act", lambda e, nout=nout: e.activation(out=sa[:, 0:nout], in_=t1[0][:, 0:nout], func=ACTF.Silu), reads=[R_t1[0]], writes=[R_sa])
                            P.op("dve", lambda e, i=i, nout=nout: e.tensor_tensor(out=gT[:, i, 0:nout], in0=sa[:, 0:nout], in1=t1[1][:, 0:nout], op=ALU.mult),
                                 reads=[R_sa, R_t1[1]], pwrites=[R_gT])
                        for s0 in range(0, nout, 128):
                            ns = min(128, nout - s0)
                            row0 = row_base + blk0 + s0
                            oi = ob_ % 2
                            xr = xr2[oi]
                            Rxr = R_xr2[oi]
                            xw = xr
                            Rxw = Rxr
                            ob_ += 1
                            rr = [hres(xmid, r) for r in range((row0 // 128) * 128, row0 + ns, 128)]
                            P.dma("sync", "xr%d" % oi, lambda e, xr=xr, row0=row0, ns=ns: e.dma_start(out=xr[0:ns, :], in_=xmid.ap()[row0:row0 + ns, :]), reads=rr, writes=[Rxr])
                            for hf in range(2):
                                fstep()
                                fstep()
                                if hf == 0:
                                    pY, RY = pview(7, F32, [512]), R_Y
                                else:
                                    pY, RY = pview(4 + (ob_ % 2), F32, [512]), R_O[ob_ % 2]
                                for c in range(NFC):
                                    P.op("pe", lambda e, c=c, hf=hf, pY=pY, s0=s0, ns=ns: e.matmul(pY[0:ns, :], lhsT=gT[:, c, s0:s0 + ns], rhs=wd[:, c, hf * 512:(hf + 1) * 512],
                                                                                                 start=(c == 0), stop=(c == NFC - 1)),
                                         reads=[R_gT, R_wd], pwrites=[RY])
                                P.op("dve", lambda e, hf=hf, pY=pY, xw=xw, xr=xr, ns=ns: e.tensor_tensor(out=xw[0:ns, hf * 512:(hf + 1) * 512], in0=pY[0:ns, :], in1=xr[0:ns, hf * 512:(hf + 1) * 512], op=ALU.add),
                                     reads=[RY, Rxr], pwrites=[Rxw])
                            if dst_t is not None:
                                wr = [hres(dst_t, r) for r in range((row0 // 128) * 128, row0 + ns, 128)]
                                P.dma("pool", "so%d" % oi, lambda e, xw=xw, row0=row0, ns=ns: e.dma_start(out=dst_t.ap()[row0:row0 + ns, :], in_=xw[0:ns, :]), reads=[Rxw], pwrites=wr)
                            else:
                                fr.final_norm_store(xw, Rxw, gfin, R_fs, row0, ns, "so%d" % oi, sa_bf, R_sa)
                        fflush()
                    ffn_tail = [R_wu, R_wd, R_fs, fr.RxnT, fr.Rxn, fr.Rjunk, fr.Rst, R_sa, R_gT] + R_x2T_b + fr.Rx + wl.R + R_hb + R_t1 + R_xr2
                    prev_tail[0] = ffn_tail

        try:
            build_body()
        except _Stop:
            pass
        P.emit(final_waits=stores)
    return nc


_CACHE = {}


def _get_nc(S_list):
    key = tuple(S_list)
    if key not in _CACHE:
        _CACHE[key] = build_program(list(S_list))
    return _CACHE[key]


def _pcol(v, k):
    Lc = v.shape[0]
    return np.ascontiguousarray(v.reshape(Lc, k, 128).transpose(0, 2, 1)).astype(np.float32)


def kernel(x_prompt, x_sample, g_attn, w_in, a_sink, b_q_norm, b_w_q_up, b_kv_norm, b_w_kv_up,
           c_q_norm, c_k_norm, d_lambda_q1, d_lambda_k1, d_lambda_q2, d_lambda_k2, d_sub_norm, w_out,
           g_ffn, w_up, b_up, conv_w, conv_b, w_down, g_final):
    f = lambda a: np.ascontiguousarray(np.asarray(a, dtype=np.float32))
    x_prompt, x_sample = f(x_prompt), f(x_sample)
    nb = x_prompt.shape[0]
    S_list = [x_prompt.shape[1], x_sample.shape[1]]
    nc = _get_nc(S_list)
    consts = make_consts(max(S_list))
    conv_w_f = f(conv_w)
    cwp = np.stack([_pcol(conv_w_f[:, j, :], 44) for j in range(3)], axis=2)
    shared = {
        "w_in": f(w_in), "w_out": f(w_out), "w_up": f(w_up), "w_down": f(w_down),
        "b_w_q_up": f(b_w_q_up), "b_w_kv_up": f(b_w_kv_up),
        "g_attn_p": _pcol(f(g_attn), 8), "g_ffn_p": _pcol(f(g_ffn), 8),
        "bqn_p": _pcol(f(b_q_norm), 2), "bkvn_p": _pcol(f(b_kv_norm), 1),
        "b_up_p": _pcol(f(b_up), 44), "conv_w_p": np.ascontiguousarray(cwp.reshape(cwp.shape[0], 128, 132)),
        "conv_b_p": _pcol(f(conv_b), 44),
        "a_sink": f(a_sink), "c_q_norm": f(c_q_norm), "c_k_norm": f(c_k_norm), "d_sub_norm": f(d_sub_norm),
        "lamv": np.ascontiguousarray(np.concatenate([f(d_lambda_q1), f(d_lambda_k1), f(d_lambda_q2), f(d_lambda_k2)], axis=1)),
        "g_final": f(g_final).reshape(1, D),
    }
    shared.update(consts)
    in_maps = []
    for b in range(nb):
        m = dict(shared)
        m["x"] = np.ascontiguousarray(np.concatenate([x_prompt[b], x_sample[b]], axis=0))
        in_maps.append(m)
    res = run_bass_kernel_spmd(nc, in_maps, core_ids=list(range(nb)))
    ys = [np.asarray(r["y"], dtype=np.float32) for r in res.results]
    yp = np.stack([y[:S_list[0]] for y in ys], axis=0)
    ysmp = np.stack([y[S_list[0]:] for y in ys], axis=0)
    return (yp, ysmp)
```
